# Optimizing a Trainium2 kernel written in Bass

```python
import jax, jax.numpy as jnp
from jax import lax
import numpy as np

D_MODEL = 2048
BATCH = 1
SEQ = 8192
DEPTH = 1
DEC_BATCH = 32
DEC_SEQ = 4
PAST_LEN = 8192
PAGE_SIZE = 128

HEAD_DIM = 128
HEADS_PER_GROUP = 4
DILATED_GROUPS = ((128, 1), (512, 4), (2048, 16))
N_GROUPS = len(DILATED_GROUPS)
N_ATT_HEADS = N_GROUPS * HEADS_PER_GROUP
ATT_WIDTH = N_ATT_HEADS * HEAD_DIM
ATT_OUT_WIDTH = HEADS_PER_GROUP * HEAD_DIM
Q_BLOCK = 128
CONV_CH = D_MODEL // 2
CONV_WIDTH = 31
N_BUCKETS = 32
MAX_DISTANCE = 2048
D_FF = 5632
N_MOD = 9
EPS = 1e-6
IN_WIDTH = 2 * CONV_CH + 3 * ATT_WIDTH + 2 * D_MODEL
NEG = -1e30

kernel_name = 'macaron_conv_dilated_attn_hybrid_step'


def _t5_buckets(dist):
    n = np.asarray(dist)
    max_exact = N_BUCKETS // 2
    large = max_exact + (np.log(np.maximum(n, 1) / max_exact) / np.log(MAX_DISTANCE / max_exact)
                         * (N_BUCKETS - max_exact)).astype(np.int64)
    large = np.minimum(large, N_BUCKETS - 1)
    return np.where(n < max_exact, n, large).astype(np.int32)


def _group_biases(rel_bias):
    out = []
    for g, (w, d) in enumerate(DILATED_GROUPS):
        b = rel_bias[jnp.asarray(_t5_buckets(d * np.arange(w // d + 1)))]
        out.append(b[:, g * HEADS_PER_GROUP:(g + 1) * HEADS_PER_GROUP].T.astype(jnp.float32))
    return out


def _rms(x, g):
    xf = x.astype(jnp.float32)
    y = xf * lax.rsqrt(jnp.mean(xf * xf, axis=-1, keepdims=True) + EPS) * g.astype(jnp.float32)
    return y.astype(x.dtype)


def _modulate(h, shift, scale):
    return h * (1 + scale) + shift


def _adaln(c, w_ada, b_ada):
    m = jax.nn.silu(c) @ w_ada + b_ada
    m = m.reshape(c.shape[0], N_MOD, 1, D_MODEL)
    return [m[:, i] for i in range(N_MOD)]


def _half_ffn(x, shift, scale, gate, n_pre, n_post, w1, w3, w2):
    h = _modulate(_rms(x, n_pre), shift, scale)
    f = (jax.nn.silu(h @ w1) * (h @ w3)) @ w2
    return x + 0.5 * gate * _rms(f, n_post)


def _mixer_in(x, shift, scale, n_pre, w_in):
    B, T, _ = x.shape
    h = _modulate(_rms(x, n_pre), shift, scale)
    p = h @ w_in
    cuts = [CONV_CH, 2 * CONV_CH, 2 * CONV_CH + ATT_WIDTH, 2 * CONV_CH + 2 * ATT_WIDTH,
            2 * CONV_CH + 3 * ATT_WIDTH, 2 * CONV_CH + 3 * ATT_WIDTH + D_MODEL]
    a, b, q, k, v, gc, ga = jnp.split(p, cuts, axis=-1)
    u = a * jax.nn.sigmoid(b)
    hs = (B, T, N_GROUPS, HEADS_PER_GROUP, HEAD_DIM)
    return u, q.reshape(hs), k.reshape(hs), v.reshape(hs), gc, ga


def _dwconv(u_ext, dw_kernel, dw_bias):
    y = lax.conv_general_dilated(u_ext, dw_kernel[:, None, :], window_strides=(1,), padding='VALID',
                                 dimension_numbers=('NWC', 'WIO', 'NWC'), feature_group_count=CONV_CH)
    return y + dw_bias


def _dilated_attend(q, ks, vs, q_idx, biases):
    B, Q = q.shape[:2]
    outs, lses = [], []
    for g, (w, d) in enumerate(DILATED_GROUPS):
        m = jnp.arange(w // d + 1)
        idx = q_idx[g][:, None] - d * m[None, :]
        valid = idx >= 0
        idx = jnp.maximum(idx, 0)
        kg = jnp.take(ks[g], idx, axis=1).astype(jnp.float32)
        vg = jnp.take(vs[g], idx, axis=1).astype(jnp.float32)
        s = jnp.einsum('bqhe,bqmhe->bqhm', q[:, :, g].astype(jnp.float32), kg) * HEAD_DIM ** -0.5
        s = jnp.where(valid[None, :, None, :], s + biases[g][None, None], NEG)
        mx = jnp.max(s, axis=-1, keepdims=True)
        p = jnp.exp(s - mx)
        den = jnp.sum(p, axis=-1)
        outs.append(jnp.einsum('bqhm,bqmhe->bqhe', p, vg) / den[..., None])
        lses.append(mx[..., 0] + jnp.log(den))
    wts = jax.nn.softmax(jnp.stack(lses, 0), axis=0)
    o = jnp.einsum('gbqh,gbqhe->bqhe', wts, jnp.stack(outs, 0))
    return o.reshape(B, Q, ATT_OUT_WIDTH).astype(q.dtype)


def _prompt_attention(q, ks, vs, biases):
    B, S = q.shape[:2]
    def blk(b):
        qb = lax.dynamic_slice_in_dim(q, b * Q_BLOCK, Q_BLOCK, axis=1)
        qi = b * Q_BLOCK + jnp.arange(Q_BLOCK)
        return _dilated_attend(qb, ks, vs, [qi] * N_GROUPS, biases)
    o = lax.map(blk, jnp.arange(S // Q_BLOCK))
    return o.transpose(1, 0, 2, 3).reshape(B, S, ATT_OUT_WIDTH)


def _mixer_out(x, gate, n_post, yconv, att_o, gc, ga, ln_g, ln_b, w_conv_out, w_att_out, w_out):
    yf = yconv.astype(jnp.float32)
    mu = jnp.mean(yf, axis=-1, keepdims=True)
    var = jnp.mean(jnp.square(yf - mu), axis=-1, keepdims=True)
    yn = ((yf - mu) * lax.rsqrt(var + EPS) * ln_g.astype(jnp.float32) + ln_b.astype(jnp.float32)).astype(x.dtype)
    conv_branch = jax.nn.silu(yn) @ w_conv_out
    att_branch = att_o @ w_att_out
    merged = jax.nn.sigmoid(gc) * conv_branch + jax.nn.sigmoid(ga) * att_branch
    return x + gate * _rms(merged @ w_out, n_post)


def setup_inputs(seed: int = 0) -> dict:
    key = jax.random.key(seed)
    ks = iter(jax.random.split(key, 48))
    def nrm(shape, s):
        return jax.random.normal(next(ks), shape, jnp.float32) * s
    def gain(shape):
        return 1.0 + nrm(shape, 0.02)
    inp = {}
    inp['x_prompt'] = nrm((BATCH, SEQ, D_MODEL), 1.0)
    inp['x_sample'] = nrm((DEC_BATCH, DEC_SEQ, D_MODEL), 1.0)
    for w, d in DILATED_GROUPS:
        L = min(w, PAST_LEN)
        inp[f'cache_k_w{w}'] = nrm((DEPTH, DEC_BATCH, L, HEADS_PER_GROUP, HEAD_DIM), 1.0)
        inp[f'cache_v_w{w}'] = nrm((DEPTH, DEC_BATCH, L, HEADS_PER_GROUP, HEAD_DIM), 1.0)
    inp['state_conv'] = nrm((DEPTH, DEC_BATCH, CONV_WIDTH - 1, CONV_CH), 0.5)
    inp['c_prompt'] = nrm((BATCH, D_MODEL), 1.0)
    inp['c_sample'] = nrm((DEC_BATCH, D_MODEL), 1.0)
    inp['w_ada'] = nrm((DEPTH, D_MODEL, N_MOD * D_MODEL), 0.5 * D_MODEL ** -0.5)
    inp['b_ada'] = nrm((DEPTH, N_MOD * D_MODEL), 0.02)
    inp['ffn1_norm_pre'] = gain((DEPTH, D_MODEL))
    inp['ffn1_norm_post'] = gain((DEPTH, D_MODEL))
    inp['ffn1_w1'] = nrm((DEPTH, D_MODEL, D_FF), D_MODEL ** -0.5)
    inp['ffn1_w3'] = nrm((DEPTH, D_MODEL, D_FF), D_MODEL ** -0.5)
    inp['ffn1_w2'] = nrm((DEPTH, D_FF, D_MODEL), D_FF ** -0.5)
    inp['mix_norm_pre'] = gain((DEPTH, D_MODEL))
    inp['mix_norm_post'] = gain((DEPTH, D_MODEL))
    inp['w_in'] = nrm((DEPTH, D_MODEL, IN_WIDTH), D_MODEL ** -0.5)
    inp['dw_kernel'] = nrm((DEPTH, CONV_WIDTH, CONV_CH), CONV_WIDTH ** -0.5)
    inp['dw_bias'] = nrm((DEPTH, CONV_CH), 0.02)
    inp['conv_ln_g'] = gain((DEPTH, CONV_CH))
    inp['conv_ln_b'] = nrm((DEPTH, CONV_CH), 0.02)
    inp['w_conv_out'] = nrm((DEPTH, CONV_CH, D_MODEL), CONV_CH ** -0.5)
    inp['w_att_out'] = nrm((DEPTH, ATT_OUT_WIDTH, D_MODEL), ATT_OUT_WIDTH ** -0.5)
    inp['w_out'] = nrm((DEPTH, D_MODEL, D_MODEL), D_MODEL ** -0.5)
    inp['rel_bias'] = nrm((N_BUCKETS, N_ATT_HEADS), 0.3)
    inp['ffn2_norm_pre'] = gain((DEPTH, D_MODEL))
    inp['ffn2_norm_post'] = gain((DEPTH, D_MODEL))
    inp['ffn2_w1'] = nrm((DEPTH, D_MODEL, D_FF), D_MODEL ** -0.5)
    inp['ffn2_w3'] = nrm((DEPTH, D_MODEL, D_FF), D_MODEL ** -0.5)
    inp['ffn2_w2'] = nrm((DEPTH, D_FF, D_MODEL), D_FF ** -0.5)
    return inp


def reference(x_prompt, x_sample, cache_k_w128, cache_v_w128, cache_k_w512, cache_v_w512,
              cache_k_w2048, cache_v_w2048, state_conv, c_prompt, c_sample,
              w_ada, b_ada, ffn1_norm_pre, ffn1_norm_post, ffn1_w1, ffn1_w3, ffn1_w2,
              mix_norm_pre, mix_norm_post, w_in, dw_kernel, dw_bias, conv_ln_g, conv_ln_b,
              w_conv_out, w_att_out, w_out, rel_bias,
              ffn2_norm_pre, ffn2_norm_post, ffn2_w1, ffn2_w3, ffn2_w2):
    biases = _group_biases(rel_bias)
    cache_k = [cache_k_w128, cache_k_w512, cache_k_w2048]
    cache_v = [cache_v_w128, cache_v_w512, cache_v_w2048]
    xp, xs = x_prompt, x_sample
    pk = [[] for _ in range(N_GROUPS)]; pv = [[] for _ in range(N_GROUPS)]
    sk = [[] for _ in range(N_GROUPS)]; sv = [[] for _ in range(N_GROUPS)]
    pconv_l, sconv_l = [], []
    for l in range(DEPTH):
        mp = _adaln(c_prompt, w_ada[l], b_ada[l])
        ms = _adaln(c_sample, w_ada[l], b_ada[l])
        f1 = (ffn1_norm_pre[l], ffn1_norm_post[l], ffn1_w1[l], ffn1_w3[l], ffn1_w2[l])
        xp = _half_ffn(xp, mp[0], mp[1], mp[2], *f1)
        xs = _half_ffn(xs, ms[0], ms[1], ms[2], *f1)
        up, qp, kp, vp, gcp, gap = _mixer_in(xp, mp[3], mp[4], mix_norm_pre[l], w_in[l])
        us, qs, ks_, vs_, gcs, gas = _mixer_in(xs, ms[3], ms[4], mix_norm_pre[l], w_in[l])
        yconv_p = _dwconv(jnp.pad(up, ((0, 0), (CONV_WIDTH - 1, 0), (0, 0))), dw_kernel[l], dw_bias[l])
        pconv_l.append(up[:, -(CONV_WIDTH - 1):])
        us_ext = jnp.concatenate([state_conv[l].astype(us.dtype), us], axis=1)
        yconv_s = _dwconv(us_ext, dw_kernel[l], dw_bias[l])
        sconv_l.append(us_ext[:, -(CONV_WIDTH - 1):])
        kp_g = [kp[:, :, g] for g in range(N_GROUPS)]
        vp_g = [vp[:, :, g] for g in range(N_GROUPS)]
        att_p = _prompt_attention(qp, kp_g, vp_g, biases)
        ks_all, vs_all, qidx_s = [], [], []
        for g, (w, d) in enumerate(DILATED_GROUPS):
            Lp = min(w, kp_g[g].shape[1])
            pk[g].append(kp_g[g][:, -Lp:]); pv[g].append(vp_g[g][:, -Lp:])
            k_all = jnp.concatenate([cache_k[g][l].astype(ks_.dtype), ks_[:, :, g]], axis=1)
            v_all = jnp.concatenate([cache_v[g][l].astype(vs_.dtype), vs_[:, :, g]], axis=1)
            qidx_s.append(cache_k[g].shape[2] + jnp.arange(xs.shape[1]))
            ks_all.append(k_all); vs_all.append(v_all)
            Ls = min(w, k_all.shape[1])
            sk[g].append(k_all[:, -Ls:]); sv[g].append(v_all[:, -Ls:])
        att_s = _dilated_attend(qs, ks_all, vs_all, qidx_s, biases)
        mo = (conv_ln_g[l], conv_ln_b[l], w_conv_out[l], w_att_out[l], w_out[l])
        xp = _mixer_out(xp, mp[5], mix_norm_post[l], yconv_p, att_p, gcp, gap, *mo)
        xs = _mixer_out(xs, ms[5], mix_norm_post[l], yconv_s, att_s, gcs, gas, *mo)
        f2 = (ffn2_norm_pre[l], ffn2_norm_post[l], ffn2_w1[l], ffn2_w3[l], ffn2_w2[l])
        xp = _half_ffn(xp, mp[6], mp[7], mp[8], *f2)
        xs = _half_ffn(xs, ms[6], ms[7], ms[8], *f2)
    pk128, pk512, pk2048 = [jnp.stack(a) for a in pk]
    pv128, pv512, pv2048 = [jnp.stack(a) for a in pv]
    sk128, sk512, sk2048 = [jnp.stack(a) for a in sk]
    sv128, sv512, sv2048 = [jnp.stack(a) for a in sv]
    pconv = jnp.stack(pconv_l)
    sconv = jnp.stack(sconv_l)
    return (xp, xs, pk128, pv128, pk512, pv512, pk2048, pv2048, pconv,
            sk128, sv128, sk512, sv512, sk2048, sv2048, sconv)
```

```python
import numpy as np
from contextlib import ExitStack
import concourse.bass as bass
import concourse.mybir as mybir
from concourse.bass_utils import run_bass_kernel_spmd

F32 = mybir.dt.float32
BF16 = mybir.dt.bfloat16
AF = mybir.ActivationFunctionType
ALU = mybir.AluOpType

NCORES = 8
D = 2048
NFT = 16
DFF = 5632
NFF = 44
TP = 1024
NHALO = 30
NS = 16
TPH = TP + NHALO
T = TPH + NS
COLT = [(0, 512), (512, 512), (1024, 46)]
CONV_CH = 1024
EPS = 1e-6
GROUPS = ((128, 1), (512, 4), (2048, 16))
HG = (128, 512, 1024)
SEND_BASE = (0, 512, 2560)
SEND_HALF = 6656
SEND_W = 2 * SEND_HALF
W_IN_OFF = dict(a=0, b=1024, q=2048, k=3584, v=5120, gc=6656, ga=8704)
V_BADA = 0
V_NORM = 144
V_DWK = 240
V_DWB = 488
V_LNG = 496
V_LNB = 504
NVEC = 512

DEBUG = False


class Buf:
    __slots__ = ("name", "w", "r", "slot", "pend")

    def __init__(self, name):
        self.name = name
        self.w = {}
        self.r = {}
        self.slot = None
        self.pend = None


class Eng:
    def __init__(self, kb, name, handle):
        self.name = name
        self.h = handle
        self.sem = kb.new_sem("e_" + name)
        self.seq = 0
        self.waited = {}
        self.pending = []


class KB:
    def __init__(self, nc):
        self.nc = nc
        self.es = ExitStack()
        self.nsem = 0
        self.slots = []
        self.free = {}
        self.holders = []
        self.pe = Eng(self, "pe", nc.tensor)
        self.act = Eng(self, "act", nc.scalar)
        self.dve = Eng(self, "dve", nc.vector)
        self.pool = Eng(self, "pool", nc.gpsimd)
        self.sp = Eng(self, "sp", nc.sync)
        self.engs = [self.pe, self.act, self.dve, self.pool, self.sp]
        self.n_ins = 0

    def new_sem(self, name):
        self.nsem += 1
        return self.es.enter_context(self.nc.semaphore(f"{name}_{self.nsem}"))

    def _wait(self, eng, ev):
        sem, val = ev
        k = id(sem)
        if eng.waited.get(k, 0) >= val:
            return
        if sem is eng.sem and eng is self.pe:
            return
        eng.h.wait_ge(sem, val)
        eng.waited[k] = val

    def _deps(self, eng, reads, writes, partial=False):
        for b in reads:
            assert b.pend is None or b.pend is eng, (b.name, "pending unsignaled access")
            for ev in b.w.values():
                self._wait(eng, ev)
        for b in writes:
            assert b.pend is None or b.pend is eng, (b.name, "pending unsignaled access")
            if not partial:
                for ev in b.w.values():
                    self._wait(eng, ev)
            for ev in b.r.values():
                self._wait(eng, ev)

    @staticmethod
    def _rec(ev, reads, writes, partial=False):
        k = id(ev[0])
        for b in writes:
            if partial:
                b.w[k] = ev
            else:
                b.w = {k: ev}
            b.r = {}
            b.pend = None
        for b in reads:
            if b in writes:
                continue
            b.r[k] = ev
            b.pend = None

    def op(self, eng, fn, reads=(), writes=(), signal=True):
        self._deps(eng, reads, writes)
        ins = fn()
        self.n_ins += 1
        if signal:
            eng.seq += 1
            ins.then_inc(eng.sem, 1)
            ev = (eng.sem, eng.seq)
            for (rs, ws) in eng.pending:
                self._rec(ev, rs, ws)
            eng.pending = []
            self._rec(ev, reads, writes)
        else:
            eng.pending.append((reads, writes))
            for b in list(reads) + list(writes):
                b.pend = eng
        return ins

    def _slot(self, sb, kind):
        if sb.slot is None:
            fl = self.free.setdefault(kind, [])
            if fl:
                sb.slot = fl.pop()
            else:
                sb.slot = [self.new_sem("d" + kind), 0, kind]
                self.slots.append(sb.slot)
            self.holders.append(sb)
        assert sb.slot[2] == kind, (sb.name, sb.slot[2], kind)
        return sb.slot

    def dma(self, q, out, in_, reads=(), writes=(), sem_buf=None, partial=False):
        self._deps(q, reads, writes, partial=partial)
        sl = self._slot(sem_buf, q.name)
        sl[1] += 16
        ins = q.h.dma_start(out=out, in_=in_).then_inc(sl[0], 16)
        self.n_ins += 1
        self._rec((sl[0], sl[1]), reads, writes, partial=partial)
        return ins

    def custom(self, eng, fn, reads, writes, sem_buf, inc):
        self._deps(eng, reads, writes)
        sl = self._slot(sem_buf, "cc")
        sl[1] += inc
        ins = fn().then_inc(sl[0], inc)
        self._rec((sl[0], sl[1]), reads, writes)
        return ins

    def group_total(self, sem_buf, bufs):
        sl = sem_buf.slot
        ev = (sl[0], sl[1])
        k = id(sl[0])
        for b in bufs:
            if k in b.w:
                b.w[k] = ev
            if k in b.r:
                b.r[k] = ev

    def barrier(self):
        for e in self.engs:
            assert not e.pending, e.name
        evs = [(e.sem, e.seq) for e in self.engs if e.seq > 0]
        evs += [(sl[0], sl[1]) for sl in self.slots if sl[1] > 0]
        for e in self.engs:
            for ev in evs:
                if ev[0] is e.sem:
                    continue
                self._wait(e, ev)
        for b in self.holders:
            self.free[b.slot[2]].append(b.slot)
            b.slot = None
        self.holders = []


class Ring:
    SLOT = 4096

    def __init__(self, cx, nslots=6):
        self.cx = cx
        self.n = nslots
        self.t = cx.gsb("wring", [128, nslots * self.SLOT], BF16)
        self.bufs = [Buf(f"ring{i}") for i in range(nslots)]
        self.busy = [False] * nslots
        self.pos = 0

    def try_load(self, src3, kt, ncols, nsl):
        cands = [p for p in range(0, self.n - nsl + 1) if p % nsl == 0]
        cands = [p for p in cands if p >= self.pos] + [p for p in cands if p < self.pos]
        pos = None
        for p in cands:
            if not any(self.busy[s] for s in range(p, p + nsl)):
                pos = p
                break
        if pos is None:
            return None
        sl = list(range(pos, pos + nsl))
        for s in sl:
            self.busy[s] = True
        self.pos = (pos + nsl) % self.n
        assert kt * ncols <= nsl * self.SLOT
        view = self.t[:, pos * self.SLOT: pos * self.SLOT + kt * ncols].rearrange("p (k n) -> p k n", k=kt)
        bufs = [self.bufs[s] for s in sl]
        kb = self.cx.kb
        kb.dma(kb.pool, view, src3, reads=[], writes=bufs, sem_buf=bufs[0])
        return (view, bufs, sl)

    def release(self, h):
        for s in h[2]:
            self.busy[s] = False


class WStream:
    def __init__(self, ring, specs, max_la=4):
        self.ring = ring
        self.specs = specs
        self.h = [None] * len(specs)
        self.nxt = 0
        self.max_la = max_la

    def get(self, i):
        while self.nxt < len(self.specs) and self.nxt <= i + self.max_la:
            h = self.ring.try_load(*self.specs[self.nxt])
            if h is None:
                break
            self.h[self.nxt] = h
            self.nxt += 1
        assert self.h[i] is not None, ("ring deadlock", i)
        return self.h[i]

    def done(self, i):
        self.ring.release(self.h[i])


def wspec(w, c0, ncols, nsl=1):
    K = w.shape[0]
    kt = K // 128
    return (w[:, c0:c0 + ncols].rearrange("(k p) n -> p k n", p=128), kt, ncols, nsl)


class Cx:
    pass


def build_program():
    nc = bass.Bass("TRN2", target_bir_lowering=False)
    cx = Cx()
    cx.nc = nc
    kb = cx.kb = KB(nc)
    es = kb.es
    pe, act, dve, pool, sp = kb.pe, kb.act, kb.dve, kb.pool, kb.sp
    V, S, G, PEn = nc.vector, nc.scalar, nc.gpsimd, nc.tensor

    _cnt = [0]

    def SBT(name, shape, dt):
        _cnt[0] += 1
        return nc.sbuf_tensor(f"{name}_u{_cnt[0]}", shape, dt)

    def din(name, shape, dt=F32):
        return nc.dram_tensor(name, list(shape), dt, kind="ExternalInput").ap()

    def dout(name, shape, dt=F32):
        return nc.dram_tensor(name, list(shape), dt, kind="ExternalOutput").ap()

    def dscr(name, shape, dt=F32):
        return nc.dram_tensor(name, list(shape), dt, kind="Internal").ap()

    def gsb(name, shape, dt):
        return es.enter_context(SBT("g_" + name, list(shape), dt))
    cx.gsb = gsb

    xin = din("xin", [T, D])
    cT_d = din("cT", [128, 80])
    vecs_d = din("vecs", [128, NVEC])
    ident_d = din("ident", [128, 128])
    sel_d = din("sel", [32, 3 * 129])
    relb_d = din("relb", [32, 12])
    flags_d = din("flags", [128, 4])
    masks_d = din("masks", [16, 32])
    ck_d = [din(f"ck{g}", [4, GROUPS[g][0], 512]) for g in range(3)]
    cv_d = [din(f"cv{g}", [4, GROUPS[g][0], 512]) for g in range(3)]
    state_d = din("state", [4, 30, CONV_CH])
    w_ada = din("w_ada", [D, 9 * D])
    w1 = [din("w1a", [D, DFF]), din("w1b", [D, DFF])]
    w3 = [din("w3a", [D, DFF]), din("w3b", [D, DFF])]
    w2 = [din("w2a", [DFF, D]), din("w2b", [DFF, D])]
    w_in = din("w_in", [D, 10752])
    w_co = din("w_co", [CONV_CH, D])
    w_ao = din("w_ao", [512, D])
    w_out = din("w_out", [D, D])

    y_o = dout("y", [TP + NS, D])
    kout = dout("kout", [3, TP, 512])
    vout = dout("vout", [3, TP, 512])
    pconv_o = dout("pconv", [NHALO, CONV_CH])
    sk_o = [dout(f"sk{g}", [4, GROUPS[g][0], 512]) for g in range(3)]
    sv_o = [dout(f"sv{g}", [4, GROUPS[g][0], 512]) for g in range(3)]
    sconv_o = dout("sconv", [4, 30, CONV_CH])

    XT = dscr("XT", [NFT, 128, T])
    FD = dscr("FD", [NFT, 128, T])
    QKV = dscr("QKV", [36, 128, T], BF16)
    send = dscr("send", [128, SEND_W], BF16)
    gath = dscr("gath", [NCORES * 128, SEND_W], BF16)
    big_d = nc.dram_tensor("bigE", [12, 130, 384], F32, kind="Internal")
    rowm_d = din("rowm", [12, 3])
    HD = dscr("HD", [NFT, 128, T], BF16)
    HDb = [Buf(f"HD{i}") for i in range(NFT)]
    XTb = [Buf(f"XT{i}") for i in range(NFT)]
    FDb = [Buf(f"FD{i}") for i in range(NFT)]
    QKVb = [Buf(f"QKV{i}") for i in range(36)]
    halo1 = dscr("halo1", [128, SEND_W], BF16)
    halo2 = dscr("halo2", [128, SEND_W], BF16)
    halo1B = Buf("halo1")
    halo2B = Buf("halo2")
    sendB = Buf("send")
    gathB = Buf("gath")
    bigB = Buf("bigE")
    outB = Buf("outs")
    dbg = {}

    PS = [es.enter_context(nc.psum_tensor(f"ps{i}", [128, 512], F32)) for i in range(7)]
    PSb = [Buf(f"ps{i}") for i in range(7)]
    PB = es.enter_context(nc.psum_tensor("psb", [128, 1024], BF16))
    _pbb = Buf("psb")
    PBb = [_pbb, _pbb]

    class Rot:
        def __init__(self, idx):
            self.idx = idx
            self.i = 0

        def next(self):
            j = self.idx[self.i % len(self.idx)]
            self.i += 1
            return PS[j], PSb[j]

    ident = gsb("ident", [128, 128], F32); identB = Buf("ident")
    identb = gsb("identb", [128, 128], BF16); identbB = Buf("identb")
    ones_bf = gsb("ones_bf", [128, 128], BF16); onesB = Buf("ones")
    vecs = gsb("vecs", [128, NVEC], F32); vecsB = Buf("vecs")
    Mt = gsb("Mt", [128, 144, 5], F32); MB = Buf("M")
    sc = gsb("sc", [128, 80], BF16); scB = Buf("sc")
    flags = gsb("flags", [128, 4], F32); flagsB = Buf("flags")
    epsT = gsb("epsT", [128, 1], F32); epsB = Buf("eps")
    rstd = gsb("rstd", [128, T], F32); rstdB = Buf("rstd")
    ring = Ring(cx, 6)
    cx.ring = ring

    def mm(out, lhsT, rhs, start, stop, reads, writes, signal):
        return kb.op(pe, lambda: PEn.matmul(out, lhsT=lhsT, rhs=rhs, start=start, stop=stop), reads, writes, signal=signal)

    def tr(out, in_, idt, reads, writes, signal=True):
        return kb.op(pe, lambda: PEn.transpose(out=out, in_=in_, identity=idt), reads, writes, signal=signal)

    kb.dma(sp, ident[:], ident_d, writes=[identB], sem_buf=identB)
    kb.dma(sp, vecs[:], vecs_d, writes=[vecsB], sem_buf=vecsB)
    kb.dma(sp, flags[:], flags_d, writes=[flagsB], sem_buf=flagsB)
    kb.op(dve, lambda: V.tensor_copy(out=identb[:], in_=ident[:]), [identB], [identbB])
    kb.op(dve, lambda: V.memset(ones_bf[:], 1.0), [], [onesB])
    kb.op(dve, lambda: V.memset(epsT[:], EPS), [], [epsB])

    def stats_to_rstd(nfeat):
        for ci, (c0, n) in enumerate(COLT):
            kb.op(act, lambda ci=ci, c0=c0, n=n: S.activation(out=rstd[:, c0:c0 + n], in_=PS[4 + ci][:, 0:n], func=AF.Sqrt,
                                                             bias=epsT[:, 0:1], scale=1.0 / nfeat),
                  [PSb[4 + ci], epsB], [rstdB])
        kb.op(dve, lambda: V.reciprocal(out=rstd[:], in_=rstd[:]), [rstdB], [rstdB])

    def stats_accum(sq_ap, sqB, i, n_tiles):
        for ci, (c0, n) in enumerate(COLT):
            mm(PS[4 + ci][:, 0:n], ones_bf[:], sq_ap[:, c0:c0 + n], i == 0, i == n_tiles - 1,
               [onesB, sqB], [PSb[4 + ci]], signal=(i == n_tiles - 1 or ci == 2))

    with ExitStack() as ph:
        def psb(name, shape, dt):
            return ph.enter_context(SBT("p_" + name, list(shape), dt))
        cTt = psb("cTt", [128, 80], F32); cTB = Buf("cT")
        kb.dma(sp, cTt[:], cT_d, writes=[cTB], sem_buf=cTB)
        kb.op(act, lambda: S.activation(out=sc[:], in_=cTt[:], func=AF.Silu), [cTB], [scB])

        relb = psb("relb", [32, 12], F32); relbB = Buf("relb")
        sel = psb("sel", [32, 3 * 129], F32); selB = Buf("sel")
        bv = psb("bv", [12, 384], F32); bvB = Buf("bv")
        bt = psb("bt", [12, 3 * 129], F32); btB = Buf("bt")
        rowm = psb("rowm", [12, 3], F32); rowmB = Buf("rowm")
        kb.dma(sp, relb[:], relb_d, writes=[relbB], sem_buf=relbB)
        kb.dma(sp, sel[:], sel_d, writes=[selB], sem_buf=selB)
        kb.dma(sp, rowm[:], rowm_d, writes=[rowmB], sem_buf=rowmB)
        kb.op(dve, lambda: V.memset(bv[:], 0.0), [], [bvB])
        mm(PS[0][0:12, 0:387], relb[:, :], sel[:, :], True, True, [relbB, selB], [PSb[0]], signal=True)
        kb.op(act, lambda: S.activation(out=bt[:, :], in_=PS[0][0:12, 0:387], func=AF.Exp), [PSb[0]], [btB])
        kb.op(dve, lambda: V.tensor_scalar(out=bv[:, 0:129], in0=bt[:, 0:129], scalar1=rowm[:, 0:1], scalar2=0.0,
                                           op0=ALU.mult, op1=ALU.add), [btB, rowmB, bvB], [bvB])
        for g in (1, 2):
            kb.op(dve, lambda g=g: V.scalar_tensor_tensor(out=bv[:, 0:129], in0=bt[:, g * 129:(g + 1) * 129],
                                                          scalar=rowm[:, g:g + 1], in1=bv[:, 0:129],
                                                          op0=ALU.mult, op1=ALU.add), [btB, rowmB, bvB], [bvB])
        kb.dma(sp, big_d.ap(), bv[:, :].unsqueeze(1).to_broadcast([12, 130, 384]), reads=[bvB], writes=[bigB], sem_buf=bvB)

        xT = psb("xT", [128, NFT, T], F32); xTB = Buf("xTall")
        xtok = [psb(f"xtok{i}", [128, D], F32) for i in range(3)]
        xtokB = [Buf(f"xtok{i}") for i in range(3)]
        sq0 = [psb(f"sq0_{i}", [128, T], BF16) for i in range(2)]
        sq0B = [Buf(f"sq0_{i}") for i in range(2)]
        rot = Rot([0, 1, 2, 3])
        ev = 0
        for tt in range(9):
            r0 = tt * 128
            nr = min(128, T - r0)
            xb_, xbB = xtok[tt % 3], xtokB[tt % 3]
            kb.dma(sp, xb_[0:nr, :], xin[r0:r0 + nr, :], writes=[xbB], sem_buf=xbB)
            for i0 in range(0, NFT, 4):
                bank, bankB = rot.next()
                for ii in range(4):
                    i = i0 + ii
                    tr(bank[:, ii * 128: ii * 128 + nr], xb_[0:nr, i * 128:(i + 1) * 128], ident[0:nr, 0:nr],
                       [xbB, identB], [bankB], signal=(ii == 3))
                src = bank[:, :].rearrange("p (a b) -> p a b", a=4)[:, :, 0:nr]
                dst = xT[:, i0:i0 + 4, r0:r0 + nr]
                if ev % 2 == 0:
                    kb.op(act, lambda src=src, dst=dst: S.copy(out=dst, in_=src), [bankB], [xTB])
                else:
                    kb.op(dve, lambda src=src, dst=dst: V.tensor_copy(out=dst, in_=src), [bankB], [xTB])
                ev += 1
        for i in range(NFT):
            kb.dma(sp, XT[i], xT[:, i, :], reads=[xTB], writes=[XTb[i]], sem_buf=xTB)
            sq_, sqB_ = sq0[i % 2], sq0B[i % 2]
            kb.op(act, lambda i=i, sq_=sq_: S.activation(out=sq_[:], in_=xT[:, i, :], func=AF.Square), [xTB], [sqB_])
            stats_accum(sq_, sqB_, i, NFT)
        kb.group_total(xTB, XTb + [xTB])
        stats_to_rstd(D)
        kb.barrier()

    ada_specs = [wspec(w_ada, 256 * c, 256) for c in range(72)]
    ada_stream = WStream(ring, ada_specs, max_la=1)
    ada_rot = Rot([6])

    def ada_chunk(c):
        view, wb, _ = ada_stream.get(c)
        bank, bankB = ada_rot.next()
        for jj in range(2):
            for kt in range(16):
                mm(bank[:, jj * 5:(jj + 1) * 5], view[:, kt, jj * 128:(jj + 1) * 128], sc[:, kt * 5:(kt + 1) * 5],
                   kt == 0, kt == 15, wb + [scB], [bankB], signal=(kt == 15))
        ada_stream.done(c)
        j0 = 2 * c
        kb.op(dve, lambda: V.tensor_tensor(out=Mt[:, j0:j0 + 2, :], in0=bank[:, 0:10].rearrange("p (a b) -> p a b", a=2),
                                           in1=vecs[:, V_BADA + j0:V_BADA + j0 + 2].unsqueeze(2).to_broadcast([128, 2, 5]),
                                           op=ALU.add), [bankB, vecsB], [MB])

    mods_t = gsb("mods", [128, 3, NFT], F32); modsB = Buf("mods")
    modS_t = gsb("modS", [128, 3, NFT, NS], F32); modSB = Buf("modS")
    mtmp = gsb("mtmp", [128, NFT, 4], F32); mtmpB = Buf("mtmp")

    def build_mods(sub, coef):
        jsh, jsc, jgt = (3 * sub) * 16, (3 * sub + 1) * 16, (3 * sub + 2) * 16
        gpre = vecs[:, V_NORM + (2 * sub) * 16: V_NORM + (2 * sub) * 16 + 16]
        gpost = vecs[:, V_NORM + (2 * sub + 1) * 16: V_NORM + (2 * sub + 1) * 16 + 16]
        R_ = [MB, vecsB]
        kb.op(dve, lambda: V.scalar_tensor_tensor(out=mods_t[:, 0, :], in0=Mt[:, jsc:jsc + 16, 0], scalar=1.0, in1=gpre,
                                                  op0=ALU.add, op1=ALU.mult), R_, [modsB])
        kb.op(dve, lambda: V.tensor_copy(out=mods_t[:, 1, :], in_=Mt[:, jsh:jsh + 16, 0]), R_ + [modsB], [modsB])
        kb.op(dve, lambda: V.scalar_tensor_tensor(out=mods_t[:, 2, :], in0=Mt[:, jgt:jgt + 16, 0], scalar=coef, in1=gpost,
                                                  op0=ALU.mult, op1=ALU.mult), R_ + [modsB], [modsB])
        def expand(k):
            kb.op(dve, lambda: V.tensor_copy(out=modS_t[:, k, :, :].rearrange("p t (b i) -> p t b i", b=4),
                                             in_=mtmp[:, :, :].unsqueeze(3).to_broadcast([128, NFT, 4, 4])),
                  [mtmpB, modSB], [modSB])
        kb.op(dve, lambda: V.scalar_tensor_tensor(out=mtmp[:, :, :], in0=Mt[:, jsc:jsc + 16, 1:5], scalar=1.0,
                                                  in1=gpre.unsqueeze(2).to_broadcast([128, NFT, 4]),
                                                  op0=ALU.add, op1=ALU.mult), R_ + [mtmpB], [mtmpB])
        expand(0)
        kb.op(dve, lambda: V.tensor_copy(out=mtmp[:, :, :], in_=Mt[:, jsh:jsh + 16, 1:5]), R_ + [mtmpB, modSB], [mtmpB])
        expand(1)
        kb.op(dve, lambda: V.scalar_tensor_tensor(out=mtmp[:, :, :], in0=Mt[:, jgt:jgt + 16, 1:5], scalar=coef,
                                                  in1=gpost.unsqueeze(2).to_broadcast([128, NFT, 4]),
                                                  op0=ALU.mult, op1=ALU.mult), R_ + [mtmpB, modSB], [mtmpB])
        expand(2)

    def prologue(ph, h, hB):
        xs = [ph.enter_context(SBT(f"pxs{i}", [128, T], F32)) for i in range(2)]
        xsB = [Buf(f"pxs{i}") for i in range(2)]
        for i in range(NFT):
            x_, xB_ = xs[i % 2], xsB[i % 2]
            kb.dma(sp, x_[:], XT[i], reads=[XTb[i]], writes=[xB_], sem_buf=xB_)
            kb.op(dve, lambda x_=x_: V.tensor_tensor(out=x_[:], in0=x_[:], in1=rstd[:], op=ALU.mult), [xB_, rstdB], [xB_])
            kb.op(act, lambda x_=x_, i=i: S.activation(out=h[:, i, 0:TPH], in_=x_[:, 0:TPH], func=AF.Identity,
                                                       bias=mods_t[:, 1, i:i + 1], scale=mods_t[:, 0, i:i + 1]),
                  [xB_, modsB], [hB[i]])
            kb.op(dve, lambda x_=x_, i=i: V.tensor_tensor(out=x_[:, TPH:T], in0=x_[:, TPH:T], in1=modS_t[:, 0, i, :], op=ALU.mult),
                  [xB_, modSB], [xB_])
            kb.op(dve, lambda x_=x_, i=i: V.tensor_tensor(out=h[:, i, TPH:T], in0=x_[:, TPH:T], in1=modS_t[:, 1, i, :], op=ALU.add),
                  [xB_, modSB, hB[i]], [hB[i]])

    class FSink:
        def __init__(self, ph):
            self.fst = [ph.enter_context(SBT(f"fst{i}", [128, T], F32)) for i in range(2)]
            self.fstB = [Buf(f"fst{i}") for i in range(2)]
            self.sq = [ph.enter_context(SBT(f"fsq{i}", [128, T], BF16)) for i in range(2)]
            self.sqB = [Buf(f"fsq{i}") for i in range(2)]
            self.deferred = None

        def buf(self, m):
            return self.fst[m % 2], self.fstB[m % 2]

        def finish_tile(self, m):
            f_, fB_ = self.buf(m)
            sq_, sqB_ = self.sq[m % 2], self.sqB[m % 2]
            kb.dma(sp, FD[m], f_[:], reads=[fB_], writes=[FDb[m]], sem_buf=fB_)
            kb.op(act, lambda: S.activation(out=sq_[:], in_=f_[:], func=AF.Square), [fB_], [sqB_])
            self.flush()
            self.deferred = (sq_, sqB_, m)

        def flush(self):
            if self.deferred is not None:
                sq_, sqB_, m = self.deferred
                stats_accum(sq_, sqB_, m, NFT)
                self.deferred = None

    def epilogue(final=False, dump=None):
        with ExitStack() as ph:
            stats_to_rstd(D)
            xs = [ph.enter_context(SBT(f"exs{i}", [128, T], F32)) for i in range(2)]
            xsB = [Buf(f"exs{i}") for i in range(2)]
            fs = [ph.enter_context(SBT(f"efs{i}", [128, T], F32)) for i in range(2)]
            fsB = [Buf(f"efs{i}") for i in range(2)]
            sq = [ph.enter_context(SBT(f"esq{i}", [128, T], BF16)) for i in range(2)]
            sqB = [Buf(f"esq{i}") for i in range(2)]
            if final:
                yst = ph.enter_context(SBT("yst", [128, 9, D], F32))
                ystB = Buf("yst")
                rot = Rot([0, 1, 2, 3])
            pend = None
            for i in range(NFT):
                x_, xB_, f_, fB_ = xs[i % 2], xsB[i % 2], fs[i % 2], fsB[i % 2]
                kb.dma(sp, x_[:], XT[i], reads=[XTb[i]], writes=[xB_], sem_buf=xB_)
                kb.dma(sp, f_[:], FD[i], reads=[FDb[i]], writes=[fB_], sem_buf=fB_)
                kb.op(dve, lambda f_=f_, i=i: V.scalar_tensor_tensor(out=f_[:, 0:TPH], in0=f_[:, 0:TPH], scalar=mods_t[:, 2, i:i + 1],
                                                                    in1=rstd[:, 0:TPH], op0=ALU.mult, op1=ALU.mult),
                      [fB_, modsB, rstdB], [fB_])
                kb.op(dve, lambda f_=f_, i=i: V.tensor_tensor(out=f_[:, TPH:T], in0=f_[:, TPH:T], in1=modS_t[:, 2, i, :], op=ALU.mult),
                      [fB_, modSB], [fB_])
                kb.op(dve, lambda f_=f_: V.tensor_tensor(out=f_[:, TPH:T], in0=f_[:, TPH:T], in1=rstd[:, TPH:T], op=ALU.mult),
                      [fB_, rstdB], [fB_])
                kb.op(dve, lambda f_=f_, x_=x_: V.tensor_tensor(out=x_[:], in0=x_[:], in1=f_[:], op=ALU.add), [fB_, xB_], [xB_])
                if not final:
                    kb.dma(sp, XT[i], x_[:], reads=[xB_], writes=[XTb[i]], sem_buf=xB_)
                    if dump is not None:
                        kb.dma(sp, dump[i], x_[:], reads=[xB_], writes=[outB], sem_buf=xB_, partial=True)
                    sq_, sqB_ = sq[i % 2], sqB[i % 2]
                    kb.op(act, lambda sq_=sq_, x_=x_: S.activation(out=sq_[:], in_=x_[:], func=AF.Square), [xB_], [sqB_])
                    if pend is not None:
                        stats_accum(*pend)
                    pend = (sq_, sqB_, i, NFT)
                else:
                    for q0 in range(0, 9, 4):
                        tts = list(range(q0, min(q0 + 4, 9)))
                        bank, bankB = rot.next()
                        for jj, tt in enumerate(tts):
                            c0, n = (tt * 128, 128) if tt < 8 else (TPH, NS)
                            tr(bank[0:n, jj * 128:(jj + 1) * 128], x_[:, c0:c0 + n], ident[:, :], [xB_, identB], [bankB],
                               signal=(jj == len(tts) - 1))
                        for jj, tt in enumerate(tts):
                            n = 128 if tt < 8 else NS
                            eng = act if (tt % 2 == 0) else dve
                            if eng is act:
                                kb.op(act, lambda jj=jj, tt=tt, n=n, bank=bank: S.copy(out=yst[0:n, tt, i * 128:(i + 1) * 128],
                                                                                     in_=bank[0:n, jj * 128:(jj + 1) * 128]),
                                      [bankB], [ystB])
                            else:
                                kb.op(dve, lambda jj=jj, tt=tt, n=n, bank=bank: V.tensor_copy(out=yst[0:n, tt, i * 128:(i + 1) * 128],
                                                                                            in_=bank[0:n, jj * 128:(jj + 1) * 128]),
                                      [bankB], [ystB])
            if pend is not None:
                stats_accum(*pend)
            if final:
                kb.dma(sp, y_o[0:TP, :].rearrange("(t p) d -> p t d", p=128), yst[:, 0:8, :], reads=[ystB], writes=[outB],
                       sem_buf=ystB, partial=True)
                kb.dma(sp, y_o[TP:TP + NS, :], yst[0:NS, 8, :], reads=[ystB], writes=[outB], sem_buf=ystB, partial=True)
            else:
                stats_to_rstd(D)
            kb.barrier()

    d2d_jobs = []
    d2dB = Buf("d2d")

    def ffn(sub, w1d, w3d, w2d, interleave=None):
        with ExitStack() as ph:
            g = ph.enter_context(SBT("ffg", [128, NFF, T], BF16))
            gB = [Buf(f"g{i}") for i in range(NFF)]
            with ExitStack() as ph1:
                h = ph1.enter_context(SBT("ffh", [128, NFT, T], BF16))
                hB = [Buf(f"h{i}") for i in range(NFT)]
                sg = [ph1.enter_context(SBT(f"sg{i}", [128, 512], F32)) for i in range(2)]
                sgB = [Buf(f"sg{i}") for i in range(2)]
                build_mods(sub, 0.5)
                prologue(ph1, h, hB)
                specs = []
                for c in range(22):
                    specs.append(wspec(w1d, 256 * c, 256))
                    specs.append(wspec(w3d, 256 * c, 256))
                st = WStream(ring, specs, max_la=3)
                rot = Rot([0, 1, 2, 3, 4, 5])
                k = 0
                for c in range(22):
                    v1, b1, _ = st.get(2 * c)
                    v3, b3, _ = st.get(2 * c + 1)
                    for jj in range(2):
                        ff = 2 * c + jj
                        for (c0, n) in COLT:
                            bg, bgB = rot.next()
                            bu, buB = rot.next()
                            for kt in range(16):
                                mm(bg[:, 0:n], v1[:, kt, jj * 128:(jj + 1) * 128], h[:, kt, c0:c0 + n], kt == 0, kt == 15,
                                   b1 + [hB[kt]], [bgB], signal=(kt == 15))
                            for kt in range(16):
                                mm(bu[:, 0:n], v3[:, kt, jj * 128:(jj + 1) * 128], h[:, kt, c0:c0 + n], kt == 0, kt == 15,
                                   b3 + [hB[kt]], [buB], signal=(kt == 15))
                            s_, sB_ = sg[k % 2], sgB[k % 2]
                            k += 1
                            kb.op(act, lambda s_=s_, bg=bg, n=n: S.activation(out=s_[:, 0:n], in_=bg[:, 0:n], func=AF.Silu),
                                  [bgB], [sB_])
                            kb.op(dve, lambda s_=s_, bu=bu, n=n, ff=ff, c0=c0: V.tensor_tensor(out=g[:, ff, c0:c0 + n], in0=s_[:, 0:n],
                                                                                             in1=bu[:, 0:n], op=ALU.mult),
                                  [sB_, buB, gB[ff]], [gB[ff]])
                    st.done(2 * c)
                    st.done(2 * c + 1)
                    if interleave is not None:
                        interleave(c)
                    if d2d_jobs:
                        for _ in range(2):
                            if d2d_jobs:
                                o_, i_ = d2d_jobs.pop(0)
                                kb.dma(sp, o_, i_, reads=[], writes=[outB], sem_buf=d2dB, partial=True)
                kb.barrier()
            with ExitStack() as ph2:
                sink = FSink(ph2)
                specs = [wspec(w2d, 256 * c, 256, nsl=3) for c in range(8)]
                st = WStream(ring, specs, max_la=1)
                rot = Rot([0, 1, 2, 3])
                for c in range(8):
                    v2, b2, _ = st.get(c)
                    for jj in range(2):
                        m = 2 * c + jj
                        f_, fB_ = sink.buf(m)
                        for (c0, n) in COLT:
                            bank, bankB = rot.next()
                            for kt in range(NFF):
                                mm(bank[:, 0:n], v2[:, kt, jj * 128:(jj + 1) * 128], g[:, kt, c0:c0 + n], kt == 0, kt == NFF - 1,
                                   b2 + [gB[kt]], [bankB], signal=(kt == NFF - 1))
                            kb.op(act, lambda f_=f_, bank=bank, c0=c0, n=n: S.copy(out=f_[:, c0:c0 + n], in_=bank[:, 0:n]), [bankB], [fB_])
                        sink.finish_tile(m)
                    st.done(c)
                sink.flush()
                kb.barrier()

    MIX_STOP = globals().get("MIX_STOP", 9)

    def mixer():
        INV_SQRT_E = 128 ** -0.5
        gath3 = gath.rearrange("(r p) n -> r p n", p=128)
        with ExitStack() as ph:
            def psb(name, shape, dt, st=None):
                return (st or ph).enter_context(SBT("m_" + name, list(shape), dt))
            m1 = psb("m1", [128, NFT, T], BF16)
            m1B = [Buf(f"m1_{i}") for i in range(NFT)]
            ONLY_MD = globals().get("ONLY_MD", False)
            if ONLY_MD:
                ccB = Buf("cc")
                kb.custom(pool, lambda: G.collective_compute("AllGather", ALU.bypass, replica_groups=[list(range(NCORES))],
                                                             ins=[send], outs=[gath]), [sendB], [gathB], ccB, 1)
                kb.barrier()
            if not ONLY_MD:
                phh = ExitStack()
                h = psb("mxh", [128, NFT, T], BF16, phh)
                hB = [Buf(f"mh{i}") for i in range(NFT)]
                build_mods(1, 1.0)
                with ExitStack() as p0:
                    prologue(p0, h, hB)
                    hsem = Buf("hsem")
                    for i in range(NFT):
                        kb.dma(sp, HD[i], h[:, i, :], reads=[hB[i]], writes=[HDb[i]], sem_buf=hsem)
                    kb.group_total(hsem, HDb + hB)
                    kb.barrier()

                with ExitStack() as pa:
                    UY = psb("UY", [128, 9, TPH], F32, pa)
                    UYB = [Buf(f"UY{i}") for i in range(9)]
                    Us = psb("Us", [128, 8, 4, 34], F32, pa); UsB = Buf("Us")
                    stash = psb("stash", [128, 8, 46], F32, pa); stashB = Buf("stash")
                    specs = []
                    for cp in range(4):
                        specs.append(wspec(w_in, W_IN_OFF["a"] + 256 * cp, 256))
                        specs.append(wspec(w_in, W_IN_OFF["b"] + 256 * cp, 256))
                    for c in range(18):
                        specs.append(wspec(w_in, W_IN_OFF["q"] + 256 * c, 256))
                    st = WStream(ring, specs, max_la=3)
                    with ExitStack() as pa0:
                        sg = [psb(f"msg{i}", [128, 512], F32, pa0) for i in range(2)]
                        sgB = [Buf(f"msg{i}") for i in range(2)]
                        stt = psb("stt", [120, CONV_CH], F32, pa0); sttB = Buf("stt")
                        PO = psb("PO", [46, CONV_CH], F32, pa0); POB = Buf("PO")

                        kb.dma(sp, stt[:], state_d.rearrange("b r c -> (b r) c"), writes=[sttB], sem_buf=sttB)
                        rot = Rot([0, 1, 2, 3, 4, 5])
                        for ct in range(8):
                            bank, bankB = rot.next()
                            tr(bank[:, 0:120], stt[0:120, ct * 128:(ct + 1) * 128], ident[0:120, 0:120], [sttB, identB], [bankB])
                            kb.op(act, lambda ct=ct, bank=bank: S.copy(out=Us[:, ct, :, 0:30], in_=bank[:, 0:120].rearrange("p (b r) -> p b r", b=4)),
                                  [bankB], [UsB])
                        k = 0
                        for cp in range(4):
                            va, ba, _ = st.get(2 * cp)
                            vb, bb, _ = st.get(2 * cp + 1)
                            for jj in range(2):
                                ct = 2 * cp + jj
                                U = UY[:, ct + 1, :]
                                UB = UYB[ct + 1]
                                for ci, (c0, n) in enumerate(COLT):
                                    bA, bAB = rot.next()
                                    bBk, bBB = rot.next()
                                    for kt in range(16):
                                        mm(bA[:, 0:n], va[:, kt, jj * 128:(jj + 1) * 128], h[:, kt, c0:c0 + n], kt == 0, kt == 15,
                                           ba + [hB[kt]], [bAB], signal=(kt == 15))
                                    for kt in range(16):
                                        mm(bBk[:, 0:n], vb[:, kt, jj * 128:(jj + 1) * 128], h[:, kt, c0:c0 + n], kt == 0, kt == 15,
                                           bb + [hB[kt]], [bBB], signal=(kt == 15))
                                    s_, sB_ = sg[k % 2], sgB[k % 2]
                                    k += 1
                                    kb.op(act, lambda s_=s_, bBk=bBk, n=n: S.activation(out=s_[:, 0:n], in_=bBk[:, 0:n], func=AF.Sigmoid),
                                          [bBB], [sB_])
                                    if ci < 2:
                                        kb.op(dve, lambda s_=s_, bA=bA, U=U, c0=c0, n=n: V.tensor_tensor(out=U[:, 30 + c0:30 + c0 + n], in0=s_[:, 0:n],
                                                                                                     in1=bA[:, 0:n], op=ALU.mult),
                                              [sB_, bAB, UB], [UB])
                                    else:
                                        kb.op(dve, lambda s_=s_, bA=bA, U=U: V.scalar_tensor_tensor(out=U[:, 0:30], in0=bA[:, 0:30], scalar=flags[:, 2:3],
                                                                                                   in1=s_[:, 0:30], op0=ALU.mult, op1=ALU.mult),
                                              [sB_, bAB, flagsB, UB], [UB])
                                        kb.op(dve, lambda s_=s_, bA=bA, ct=ct: V.tensor_tensor(out=Us[:, ct, :, 30:34],
                                                                                              in0=s_[:, 30:46].rearrange("p (b i) -> p b i", b=4),
                                                                                              in1=bA[:, 30:46].rearrange("p (b i) -> p b i", b=4), op=ALU.mult),
                                              [sB_, bAB, UsB], [UsB])
                                kb.op(act, lambda ct=ct, U=U: S.copy(out=stash[:, ct, 0:30], in_=U[:, TPH - 30:TPH]), [UB, stashB], [stashB])
                                kb.op(act, lambda ct=ct: S.copy(out=stash[:, ct, 30:46].rearrange("p (b i) -> p b i", b=4), in_=Us[:, ct, :, 30:34]),
                                      [UsB, stashB], [stashB])
                            st.done(2 * cp)
                            st.done(2 * cp + 1)
                        for half in range(2):
                            bank, bankB = rot.next()
                            for jj in range(4):
                                ct = half * 4 + jj
                                tr(bank[0:46, jj * 128:(jj + 1) * 128], stash[:, ct, :], ident[:, :], [stashB, identB], [bankB], signal=(jj == 3))
                            kb.op(act, lambda half=half, bank=bank: S.copy(out=PO[0:46, half * 512:(half + 1) * 512], in_=bank[0:46, 0:512]), [bankB, POB], [POB])
                        kb.dma(sp, pconv_o, PO[0:30, :], reads=[POB], writes=[outB], sem_buf=POB, partial=True)
                        for b in range(4):
                            kb.dma(sp, sconv_o[b, 26:30, :], PO[30 + 4 * b:34 + 4 * b, :], reads=[POB], writes=[outB], sem_buf=POB, partial=True)

                        kb.barrier()
                    with ExitStack() as pa1:
                        qb16 = [psb(f"qb16_{i}", [128, T], BF16, pa1) for i in range(2)]
                        qb16B = [Buf(f"qb16_{i}") for i in range(2)]
                        f32s = [psb(f"f32s_{i}", [128, T], F32, pa1) for i in range(2)]
                        f32sB = [Buf(f"f32s_{i}") for i in range(2)]
                        KO = psb("KO", [128, 8, 256], F32, pa1); KOB = Buf("KO")
                        KS = psb("KS", [16, 256], F32, pa1); KSB = Buf("KS")
                        dwk = vecs[:, V_DWK:V_DWK + 248].rearrange("p (c j) -> p c j", c=8)
                        for ct in range(8):
                            U = UY[:, ct + 1, :]; UB = UYB[ct + 1]
                            Y = UY[:, ct, :]; YB = UYB[ct]
                            Ys = Y[:, TP:TP + 16].rearrange("p (b i) -> p b i", b=4)
                            kb.op(dve, lambda U=U, Y=Y, ct=ct: V.tensor_scalar(out=Y[:, 0:TP], in0=U[:, 0:TP], scalar1=dwk[:, ct, 0:1],
                                                                              scalar2=vecs[:, V_DWB + ct:V_DWB + ct + 1], op0=ALU.mult, op1=ALU.add),
                                  [UB, vecsB, YB], [YB])
                            kb.op(dve, lambda Ys=Ys, ct=ct: V.tensor_scalar(out=Ys, in0=Us[:, ct, :, 0:4], scalar1=dwk[:, ct, 0:1],
                                                                           scalar2=vecs[:, V_DWB + ct:V_DWB + ct + 1], op0=ALU.mult, op1=ALU.add),
                                  [UsB, vecsB, YB], [YB])
                            for j in range(1, 31):
                                kb.op(dve, lambda U=U, Y=Y, ct=ct, j=j: V.scalar_tensor_tensor(out=Y[:, 0:TP], in0=U[:, j:j + TP], scalar=dwk[:, ct, j:j + 1],
                                                                                              in1=Y[:, 0:TP], op0=ALU.mult, op1=ALU.add),
                                      [UB, vecsB, YB], [YB])
                                kb.op(dve, lambda Ys=Ys, ct=ct, j=j: V.scalar_tensor_tensor(out=Ys, in0=Us[:, ct, :, j:j + 4], scalar=dwk[:, ct, j:j + 1],
                                                                                           in1=Ys, op0=ALU.mult, op1=ALU.add),
                                      [UsB, vecsB, YB], [YB])

                        rotq = Rot([0, 1, 2, 3])
                        rott = Rot([4, 5])
                        for c in range(18):
                            vw, bw, _ = st.get(8 + c)
                            for jj in range(2):
                                mt = 2 * c + jj
                                which, gh = mt // 12, mt % 12
                                g_, hh = gh // 4, gh % 4
                                qb_, qbB_ = qb16[mt % 2], qb16B[mt % 2]
                                fs_, fsB_ = f32s[mt % 2], f32sB[mt % 2]
                                for (c0, n) in COLT:
                                    bank, bankB = rotq.next()
                                    for kt in range(16):
                                        mm(bank[:, 0:n], vw[:, kt, jj * 128:(jj + 1) * 128], h[:, kt, c0:c0 + n], kt == 0, kt == 15,
                                           bw + [hB[kt]], [bankB], signal=(kt == 15))
                                    if which == 0:
                                        kb.op(act, lambda qb_=qb_, bank=bank, c0=c0, n=n: S.mul(out=qb_[:, c0:c0 + n], in_=bank[:, 0:n], mul=INV_SQRT_E),
                                              [bankB, qbB_], [qbB_])
                                    else:
                                        kb.op(act, lambda fs_=fs_, bank=bank, c0=c0, n=n: S.copy(out=fs_[:, c0:c0 + n], in_=bank[:, 0:n]), [bankB, fsB_], [fsB_])
                                if which > 0:
                                    kb.op(act, lambda qb_=qb_, fs_=fs_: S.copy(out=qb_[:], in_=fs_[:]), [fsB_, qbB_], [qbB_])
                                kb.dma(sp, QKV[mt], qb_[:], reads=[qbB_], writes=[QKVb[mt]], sem_buf=qbB_)
                                if which > 0:
                                    H_ = HG[g_]
                                    off = (which - 1) * SEND_HALF + SEND_BASE[g_] + hh * H_
                                    kb.dma(sp, send[:, off:off + H_], qb_[:, TP - H_:TP], reads=[qbB_], writes=[sendB], sem_buf=qbB_, partial=True)
                                    hp, hq = hh // 2, hh % 2
                                    for q0 in (0, 4):
                                        bank, bankB = rott.next()
                                        for j4 in range(4):
                                            tt = q0 + j4
                                            tr(bank[:, j4 * 128:(j4 + 1) * 128], fs_[:, tt * 128:(tt + 1) * 128], ident[:, :], [fsB_, identB], [bankB],
                                               signal=(j4 == 3))
                                        kb.op(act, lambda bank=bank, q0=q0, hq=hq: S.copy(out=KO[:, q0:q0 + 4, hq * 128:(hq + 1) * 128],
                                                                                         in_=bank[:, 0:512].rearrange("p (a b) -> p a b", a=4)),
                                              [bankB, KOB], [KOB])
                                    bank, bankB = rott.next()
                                    tr(bank[0:16, 0:128], fs_[:, TPH:T], ident[:, :], [fsB_, identB], [bankB])
                                    kb.op(act, lambda bank=bank, hq=hq: S.copy(out=KS[0:16, hq * 128:(hq + 1) * 128], in_=bank[0:16, 0:128]), [bankB, KSB], [KSB])
                                    if hq == 1:
                                        dst = kout if which == 1 else vout
                                        kb.dma(sp, dst[g_][:, hp * 256:(hp + 1) * 256].rearrange("(t p) c -> p t c", p=128), KO[:, :, :],
                                               reads=[KOB], writes=[outB], sem_buf=KOB, partial=True)
                                        so = sk_o if which == 1 else sv_o
                                        Lg = GROUPS[g_][0]
                                        for b in range(4):
                                            kb.dma(sp, so[g_][b, Lg - 4:Lg, hp * 256:(hp + 1) * 256], KS[4 * b:4 * b + 4, :], reads=[KSB], writes=[outB],
                                                   sem_buf=KSB, partial=True)
                            st.done(8 + c)
                        ccB = Buf("cc")
                        kb.custom(pool, lambda: G.collective_compute("AllGather", ALU.bypass, replica_groups=[list(range(NCORES))],
                                                                     ins=[send], outs=[gath]), [sendB], [gathB], ccB, 1)
                        kb.barrier()
                    if MIX_STOP <= 1:
                        return False

                    Sx = psb("Sx", [128, 8, T], BF16, pa)
                    SxB = [Buf(f"Sx{i}") for i in range(8)]
                    with ExitStack() as pb:
                        mu = psb("lnmu", [128, TP + 16], F32, pb); muB = Buf("mu")
                        rs = psb("lnrs", [128, TP + 16], F32, pb); rsB = Buf("rs")
                        yb = [psb(f"lnyb{i}", [128, TP + 16], BF16, pb) for i in range(2)]
                        ybB = [Buf(f"lnyb{i}") for i in range(2)]
                        ysq = [psb(f"lnysq{i}", [128, TP + 16], BF16, pb) for i in range(2)]
                        ysqB = [Buf(f"lnysq{i}") for i in range(2)]
                        LC = [(0, 512), (512, 512), (1024, 16)]
                        for ct in range(8):
                            Y = UY[:, ct, :]; YB = UYB[ct]
                            a_, aB_, q_, qB_ = yb[ct % 2], ybB[ct % 2], ysq[ct % 2], ysqB[ct % 2]
                            kb.op(act, lambda a_=a_, Y=Y: S.copy(out=a_[:], in_=Y[:, 0:TP + 16]), [YB], [aB_])
                            kb.op(act, lambda q_=q_, Y=Y: S.activation(out=q_[:], in_=Y[:, 0:TP + 16], func=AF.Square), [YB], [qB_])
                            for ci, (c0, n) in enumerate(LC):
                                mm(PS[ci][:, 0:n], ones_bf[:], a_[:, c0:c0 + n], ct == 0, ct == 7, [onesB, aB_], [PSb[ci]], signal=False)
                                mm(PS[3 + ci][:, 0:n], ones_bf[:], q_[:, c0:c0 + n], ct == 0, ct == 7, [onesB, qB_], [PSb[3 + ci]], signal=(ci == 2))
                        for ci, (c0, n) in enumerate(LC):
                            kb.op(act, lambda ci=ci, c0=c0, n=n: S.mul(out=mu[:, c0:c0 + n], in_=PS[ci][:, 0:n], mul=1.0 / CONV_CH), [PSb[ci]], [muB])
                        kb.op(dve, lambda: V.tensor_tensor(out=rs[:], in0=mu[:], in1=mu[:], op=ALU.mult), [muB], [rsB])
                        for ci, (c0, n) in enumerate(LC):
                            kb.op(dve, lambda ci=ci, c0=c0, n=n: V.scalar_tensor_tensor(out=rs[:, c0:c0 + n], in0=PS[3 + ci][:, 0:n], scalar=1.0 / CONV_CH,
                                                                                       in1=rs[:, c0:c0 + n], op0=ALU.mult, op1=ALU.subtract),
                                  [PSb[3 + ci], rsB], [rsB])
                        kb.op(act, lambda: S.activation(out=rs[:], in_=rs[:], func=AF.Sqrt, bias=epsT[:, 0:1], scale=1.0), [rsB, epsB], [rsB])
                        kb.op(dve, lambda: V.reciprocal(out=rs[:], in_=rs[:]), [rsB], [rsB])
                        for ct in range(8):
                            Y = UY[:, ct, :]; YB = UYB[ct]
                            kb.op(dve, lambda Y=Y: V.tensor_tensor(out=Y[:, 0:TP + 16], in0=Y[:, 0:TP + 16], in1=mu[:], op=ALU.subtract), [YB, muB], [YB])
                            kb.op(dve, lambda Y=Y: V.tensor_tensor(out=Y[:, 0:TP + 16], in0=Y[:, 0:TP + 16], in1=rs[:], op=ALU.mult), [YB, rsB], [YB])
                            kb.op(dve, lambda ct=ct: V.memset(Sx[:, ct, TP:TPH], 0.0), [SxB[ct]], [SxB[ct]])
                            kb.op(act, lambda ct=ct, Y=Y: S.activation(out=Sx[:, ct, 0:TP], in_=Y[:, 0:TP], func=AF.Silu,
                                                                       bias=vecs[:, V_LNB + ct:V_LNB + ct + 1], scale=vecs[:, V_LNG + ct:V_LNG + ct + 1]),
                                  [YB, vecsB, SxB[ct]], [SxB[ct]])
                            kb.op(act, lambda ct=ct, Y=Y: S.activation(out=Sx[:, ct, TPH:T], in_=Y[:, TP:TP + 16], func=AF.Silu,
                                                                       bias=vecs[:, V_LNB + ct:V_LNB + ct + 1], scale=vecs[:, V_LNG + ct:V_LNG + ct + 1]),
                                  [YB, vecsB, SxB[ct]], [SxB[ct]])
                        kb.barrier()
                    with ExitStack() as pc:
                        sg = [psb(f"csg{i}", [128, 512], F32, pc) for i in range(2)]
                        sgB = [Buf(f"csg{i}") for i in range(2)]
                        specs = []
                        for q4 in range(4):
                            specs.append(wspec(w_co, 512 * q4, 512))
                            specs.append(wspec(w_in, W_IN_OFF["gc"] + 512 * q4, 256))
                            specs.append(wspec(w_in, W_IN_OFF["gc"] + 512 * q4 + 256, 256))
                        st = WStream(ring, specs, max_la=3)
                        rot = Rot([0, 1, 2, 3, 4, 5])
                        k = 0
                        for q4 in range(4):
                            vco, bco, _ = st.get(3 * q4)
                            for half in range(2):
                                vg, bg_, _ = st.get(3 * q4 + 1 + half)
                                for jj in range(2):
                                    mt = 4 * q4 + 2 * half + jj
                                    for (c0, n) in COLT:
                                        bC, bCB = rot.next()
                                        bG, bGB = rot.next()
                                        for kt in range(8):
                                            mm(bC[:, 0:n], vco[:, kt, (2 * half + jj) * 128:(2 * half + jj + 1) * 128], Sx[:, kt, c0:c0 + n], kt == 0, kt == 7,
                                               bco + [SxB[kt]], [bCB], signal=(kt == 7))
                                        for kt in range(16):
                                            mm(bG[:, 0:n], vg[:, kt, jj * 128:(jj + 1) * 128], h[:, kt, c0:c0 + n], kt == 0, kt == 15,
                                               bg_ + [hB[kt]], [bGB], signal=(kt == 15))
                                        s_, sB_ = sg[k % 2], sgB[k % 2]
                                        k += 1
                                        kb.op(act, lambda s_=s_, bG=bG, n=n: S.activation(out=s_[:, 0:n], in_=bG[:, 0:n], func=AF.Sigmoid), [bGB], [sB_])
                                        kb.op(dve, lambda s_=s_, bC=bC, mt=mt, c0=c0, n=n: V.tensor_tensor(out=m1[:, mt, c0:c0 + n], in0=s_[:, 0:n], in1=bC[:, 0:n],
                                                                                                         op=ALU.mult), [sB_, bCB, m1B[mt]], [m1B[mt]])
                                st.done(3 * q4 + 1 + half)
                            st.done(3 * q4)
                        kb.barrier()

                phh.close()
            if MIX_STOP <= 2:
                return False
            att = psb("att_o", [128, 4, T], BF16)
            attB = [Buf(f"att{i}") for i in range(4)]
            kb.op(dve, lambda: V.memset(att[:, :, :], 0.0), attB, attB)
            for k_, (hd, hdB) in ((1, (halo1, halo1B)), (2, (halo2, halo2B))):
                pv_ = (G.partition_id() + (8 - k_)) % 8
                kb.dma(pool, hd, gath3[bass.ds(pv_, 1), :, :], reads=[gathB], writes=[hdB], sem_buf=hdB)
            with ExitStack() as pd:
                accN = psb("accN", [128, TP], F32, pd); accNB = Buf("accN")
                accD = psb("accD", [128, TP], F32, pd); accDB = Buf("accD")
                Eh = [psb(f"Eh{i}", [128, 3, 256], F32, pd) for i in range(2)]
                EhB = [Buf(f"Eh{i}") for i in range(2)]
                Em = [psb(f"Em{i}", [128, 3, 256], F32, pd) for i in range(2)]
                EmB = [Buf(f"Em{i}") for i in range(2)]
                Em2 = [psb(f"Em2_{i}", [128, 64], F32, pd) for i in range(2)]
                Em2B = [Buf(f"Em2_{i}") for i in range(2)]
                sets = []
                for s_i in range(2):
                    d_ = {}
                    for nm, w_ in (("q", T), ("kl", T), ("vl", T), ("kh", 2048), ("vh", 2048)):
                        d_[nm] = psb(f"a{nm}{s_i}", [128, w_], BF16, pd)
                        d_[nm + "B"] = Buf(f"a{nm}{s_i}")
                    sets.append(d_)
                VT = psb("VT", [128, 48, 128], BF16, pd); VTB = Buf("VT")
                ex = [psb(f"ex{i}", [128, 128], F32, pd) for i in range(3)]
                exB = [Buf(f"ex{i}") for i in range(3)]
                pT = [psb(f"pT{i}", [128, 128], BF16, pd) for i in range(3)]
                pTB = [Buf(f"pT{i}") for i in range(3)]
                set_i = 0
                for hh in range(4):
                    E_, EB_, Em_, EmB_, E2_, E2B_ = Eh[hh % 2], EhB[hh % 2], Em[hh % 2], EmB[hh % 2], Em2[hh % 2], Em2B[hh % 2]
                    srcE = bass.AP(big_d, hh * 130 * 384, [[383, 128], [4 * 130 * 384, 3], [1, 256]])
                    kb.dma(sp, E_[:, :, :], srcE, reads=[bigB], writes=[EB_], sem_buf=EB_)
                    kb.op(dve, lambda E_=E_, Em_=Em_: V.tensor_scalar(out=Em_[:, :, :], in0=E_[:, :, :], scalar1=flags[:, 0:1], scalar2=0.0,
                                                                     op0=ALU.mult, op1=ALU.add), [EB_, flagsB, EmB_], [EmB_])
                    kb.op(dve, lambda E_=E_, E2_=E2_: V.tensor_scalar(out=E2_[:, :], in0=E_[:, 2, 128:192], scalar1=flags[:, 1:2], scalar2=0.0,
                                                                     op0=ALU.mult, op1=ALU.add), [EB_, flagsB, E2B_], [E2B_])
                    for g_ in range(3):
                        d_ = sets[set_i % 2]
                        set_i += 1
                        gh = 4 * g_ + hh
                        H_ = HG[g_]
                        dd = GROUPS[g_][1]
                        kb.dma(sp, d_["q"][:], QKV[gh], reads=[QKVb[gh]], writes=[d_["qB"]], sem_buf=d_["qB"])
                        kb.dma(sp, d_["kl"][:], QKV[12 + gh], reads=[QKVb[12 + gh]], writes=[d_["klB"]], sem_buf=d_["klB"])
                        kb.dma(sp, d_["vl"][:], QKV[24 + gh], reads=[QKVb[24 + gh]], writes=[d_["vlB"]], sem_buf=d_["vlB"])
                        for nm, half in (("kh", 0), ("vh", 1)):
                            off = half * SEND_HALF + SEND_BASE[g_] + hh * H_
                            if g_ < 2:
                                kb.dma(sp, d_[nm][:, 0:H_], halo1[:, off:off + H_], reads=[halo1B], writes=[d_[nm + "B"]],
                                       sem_buf=d_[nm + "B"])
                            else:
                                kb.dma(sp, d_[nm][:, 0:1024], halo2[:, off:off + 1024], reads=[halo2B], writes=[d_[nm + "B"]],
                                       sem_buf=d_[nm + "B"])
                                kb.dma(sp, d_[nm][:, 1024:2048], halo1[:, off:off + 1024], reads=[halo1B], writes=[d_[nm + "B"]],
                                       sem_buf=d_[nm + "B"], partial=True)
                        tiles = {}
                        blocks = []
                        if g_ == 0:
                            tiles["H"] = ("h", 0, 1, 128)
                            for j in range(8):
                                tiles[("L", j)] = ("l", 128 * j, 1, 128)
                            for j in range(8):
                                prev = ("H" if j == 0 else ("L", j - 1))
                                blocks.append((128 * j, 1, 128, [(prev, 128, "m" if j == 0 else "e"), (("L", j), 0, "e")]))
                        elif g_ == 1:
                            for r in range(4):
                                tiles[("H", r)] = ("h", r, 4, 128)
                                tiles[("L", r, 0)] = ("l", r, 4, 128)
                                tiles[("L", r, 1)] = ("l", r + 512, 4, 128)
                            for r in range(4):
                                for qb in range(2):
                                    prev = (("H", r) if qb == 0 else ("L", r, 0))
                                    blocks.append((r + 512 * qb, 4, 128, [(prev, 128, "m" if qb == 0 else "e"), (("L", r, qb), 0, "e")]))
                        else:
                            for r in range(16):
                                tiles[("H2", r)] = ("h", r, 16, 64)
                                tiles[("H1", r)] = ("h", 1024 + r, 16, 64)
                                tiles[("L", r)] = ("l", r, 16, 64)
                            for r in range(16):
                                blocks.append((r, 16, 64, [(("H2", r), 128, "m2"), (("H1", r), 64, "m"), (("L", r), 0, "e")]))
                        tkeys = list(tiles.keys())
                        slot = {tk: i for i, tk in enumerate(tkeys)}
                        for i0 in range(0, len(tkeys), 4):
                            grp = tkeys[i0:i0 + 4]
                            hb = (i0 // 4) % 2
                            for j4, tk in enumerate(grp):
                                srcn, c0, stp, nk = tiles[tk]
                                vsrc = d_["vh"] if srcn == "h" else d_["vl"]
                                vB_ = d_["vhB"] if srcn == "h" else d_["vlB"]
                                tr(PB[0:nk, hb * 512 + j4 * 128: hb * 512 + (j4 + 1) * 128], vsrc[:, c0:c0 + stp * (nk - 1) + 1:stp], identb[:, :],
                                   [vB_, identbB], [PBb[hb]], signal=(j4 == len(grp) - 1))
                            nk = tiles[grp[0]][3]
                            kb.op(act, lambda i0=i0, hb=hb, nk=nk, ng=len(grp): S.copy(out=VT[0:nk, i0:i0 + ng, :],
                                                                                       in_=PB[0:nk, hb * 512: hb * 512 + ng * 128].rearrange("p (a b) -> p a b", a=ng)),
                                  [PBb[hb], VTB], [VTB])
                        work = []
                        for bi, (qs, qstep, nq, tl) in enumerate(blocks):
                            for ti, (tk, off, esel) in enumerate(tl):
                                work.append((bi, ti, len(tl), qs, qstep, nq, tk, off, esel))
                        rS = Rot([0, 1, 2])
                        nd = [(PS[3], PSb[3], PS[4], PSb[4]), (PS[5], PSb[5], PS[6], PSb[6])]
                        sc_state = {}

                        def emit_score(wi):
                            bi, ti, nt, qs, qstep, nq, tk, off, esel = work[wi]
                            srcn, c0, stp, nk = tiles[tk]
                            ksrc = d_["kh"] if srcn == "h" else d_["kl"]
                            kB_ = d_["khB"] if srcn == "h" else d_["klB"]
                            bank, bankB = rS.next()
                            mm(bank[0:nk, 0:nq], ksrc[:, c0:c0 + stp * (nk - 1) + 1:stp], d_["q"][:, qs:qs + qstep * (nq - 1) + 1:qstep], True, True,
                               [kB_, d_["qB"]], [bankB], signal=True)
                            e_, eB_, p_, pB_ = ex[wi % 3], exB[wi % 3], pT[wi % 3], pTB[wi % 3]
                            kb.op(act, lambda: S.activation(out=e_[0:nk, 0:nq], in_=bank[0:nk, 0:nq], func=AF.Exp), [bankB], [eB_])
                            if esel == "e":
                                Es, EsB = E_[0:nk, g_, off:off + nq], EB_
                            elif esel == "m":
                                Es, EsB = Em_[0:nk, g_, off:off + nq], EmB_
                            else:
                                Es, EsB = E2_[0:nk, 0:nq], E2B_
                            kb.op(dve, lambda: V.tensor_tensor(out=p_[0:nk, 0:nq], in0=e_[0:nk, 0:nq], in1=Es, op=ALU.mult), [eB_, EsB], [pB_])
                            sc_state[wi] = (p_, pB_, nk)

                        def emit_pv(wi):
                            bi, ti, nt, qs, qstep, nq, tk, off, esel = work[wi]
                            p_, pB_, nk = sc_state.pop(wi)
                            nP, nPB, dP, dPB = nd[bi % 2]
                            mm(nP[:, 0:nq], VT[0:nk, slot[tk], :], p_[0:nk, 0:nq], ti == 0, ti == nt - 1, [VTB, pB_], [nPB], signal=False)
                            mm(dP[:, 0:nq], ones_bf[0:nk, :], p_[0:nk, 0:nq], ti == 0, ti == nt - 1, [onesB, pB_], [dPB], signal=True)
                            if ti == nt - 1:
                                oN = accN[:, qs:qs + qstep * (nq - 1) + 1:qstep]
                                oD = accD[:, qs:qs + qstep * (nq - 1) + 1:qstep]
                                if g_ == 0:
                                    kb.op(act, lambda: S.copy(out=oN, in_=nP[:, 0:nq]), [nPB, accNB], [accNB])
                                    kb.op(dve, lambda: V.tensor_copy(out=oD, in_=dP[:, 0:nq]), [dPB, accDB], [accDB])
                                else:
                                    kb.op(dve, lambda: V.tensor_tensor(out=oN, in0=nP[:, 0:nq], in1=oN, op=ALU.add), [nPB, accNB], [accNB])
                                    kb.op(dve, lambda: V.tensor_tensor(out=oD, in0=dP[:, 0:nq], in1=oD, op=ALU.add), [dPB, accDB], [accDB])

                        emit_score(0)
                        for wi in range(len(work)):
                            if wi + 1 < len(work):
                                emit_score(wi + 1)
                            emit_pv(wi)
                    kb.op(dve, lambda: V.reciprocal(out=accD[:, :], in_=accD[:, :]), [accDB], [accDB])
                    kb.op(dve, lambda hh=hh: V.tensor_tensor(out=att[:, hh, 0:TP], in0=accN[:, :], in1=accD[:, :], op=ALU.mult),
                          [accNB, accDB, attB[hh]], [attB[hh]])
                kb.barrier()

            if MIX_STOP <= 3:
                return False
            with ExitStack() as ps_:
                qkvn = psb("qkvn", [128, 36, NS], BF16, ps_); qkvnB = Buf("qkvn")
                for w3_ in range(3):
                    kb.dma(sp, qkvn[:, 12 * w3_:12 * w3_ + 12, :], QKV[12 * w3_:12 * w3_ + 12, :, TPH:T].rearrange("m p c -> p m c"),
                           reads=QKVb[12 * w3_:12 * w3_ + 12], writes=[qkvnB], sem_buf=qkvnB, partial=(w3_ > 0))
                Esm = psb("Esm", [128, 12, 4], F32, ps_); EsmB = Buf("Esm")
                En = psb("En", [16, 12, 16], F32, ps_); EnB = Buf("En")
                msk = psb("msk", [16, 32], F32, ps_); mskB = Buf("msk")
                kb.dma(sp, Esm[:, :, :], bass.AP(big_d, 128, [[383, 128], [130 * 384, 12], [1, 4]]), reads=[bigB], writes=[EsmB], sem_buf=EsmB)
                kb.dma(sp, En[:, :, :], bass.AP(big_d, 0, [[383, 16], [130 * 384, 12], [1, 16]]), reads=[bigB], writes=[EnB], sem_buf=EnB)
                kb.dma(sp, msk[:, :], masks_d, writes=[mskB], sem_buf=mskB)
                kb.op(dve, lambda: V.tensor_tensor(out=En[:, 0:4, :], in0=En[:, 0:4, :], in1=msk[:, 0:16].unsqueeze(1).to_broadcast([16, 4, 16]), op=ALU.mult),
                      [EnB, mskB], [EnB])
                kb.op(dve, lambda: V.tensor_tensor(out=En[:, 4:12, :], in0=En[:, 4:12, :], in1=msk[:, 16:32].unsqueeze(1).to_broadcast([16, 8, 16]), op=ALU.mult),
                      [EnB, mskB], [EnB])
                aN = psb("aNs", [128, 4, NS], F32, ps_); aNB = Buf("aNs")
                aD = psb("aDs", [128, 4, NS], F32, ps_); aDB = Buf("aDs")
                vnT = psb("vnT", [16, 12, 128], BF16, ps_); vnTB = Buf("vnT")
                exs = [psb(f"exs{i}", [128, 16], F32, ps_) for i in range(2)]
                exsB = [Buf(f"exs{i}") for i in range(2)]
                pTs = [psb(f"pTs{i}", [128, 16], BF16, ps_) for i in range(2)]
                pTsB = [Buf(f"pTs{i}") for i in range(2)]
                Kc = [psb(f"Kc{i}", [128, 4, 512], BF16, ps_) for i in range(2)]
                KcB = [Buf(f"Kc{i}") for i in range(2)]
                Vc = [psb(f"Vc{i}", [128, 4, 512], BF16, ps_) for i in range(2)]
                VcB = [Buf(f"Vc{i}") for i in range(2)]
                KcT = [psb(f"KcT{i}", [128, 4, 4, 128], BF16, ps_) for i in range(2)]
                KcTB = [Buf(f"KcT{i}") for i in range(2)]
                for i0 in range(0, 12, 4):
                    hb = (i0 // 4) % 2
                    for j4 in range(4):
                        tr(PB[0:16, hb * 512 + j4 * 128: hb * 512 + (j4 + 1) * 128], qkvn[:, 24 + i0 + j4, :], identb[:, :], [qkvnB, identbB], [PBb[hb]],
                           signal=(j4 == 3))
                    kb.op(act, lambda i0=i0, hb=hb: S.copy(out=vnT[0:16, i0:i0 + 4, :], in_=PB[0:16, hb * 512:hb * 512 + 512].rearrange("p (a b) -> p a b", a=4)),
                          [PBb[hb], vnTB], [vnTB])
                rS = Rot([0, 1, 2])
                rN = Rot([3, 4])
                rD = Rot([5, 6])
                first = [True] * 4
                cnt = 0

                def accum(hh, c0, n, nP, nPB, dP, dPB):
                    oN, oD = aN[:, hh, c0:c0 + n], aD[:, hh, c0:c0 + n]
                    if first[hh]:
                        kb.op(dve, lambda: V.tensor_copy(out=oN, in_=nP[:, 0:n]), [nPB, aNB], [aNB])
                        kb.op(dve, lambda: V.tensor_copy(out=oD, in_=dP[:, 0:n]), [dPB, aDB], [aDB])
                    else:
                        kb.op(dve, lambda: V.tensor_tensor(out=oN, in0=nP[:, 0:n], in1=oN, op=ALU.add), [nPB, aNB], [aNB])
                        kb.op(dve, lambda: V.tensor_tensor(out=oD, in0=dP[:, 0:n], in1=oD, op=ALU.add), [dPB, aDB], [aDB])

                for hh in range(4):
                    for g_ in range(3):
                        gh = 4 * g_ + hh
                        bank, bankB = rS.next()
                        mm(bank[0:16, 0:16], qkvn[:, 12 + gh, :], qkvn[:, gh, :], True, True, [qkvnB], [bankB], signal=True)
                        e_, eB_, p_, pB_ = exs[cnt % 2], exsB[cnt % 2], pTs[cnt % 2], pTsB[cnt % 2]
                        cnt += 1
                        kb.op(act, lambda e_=e_, bank=bank: S.activation(out=e_[0:16, 0:16], in_=bank[0:16, 0:16], func=AF.Exp), [bankB], [eB_])
                        kb.op(dve, lambda e_=e_, p_=p_, gh=gh: V.tensor_tensor(out=p_[0:16, 0:16], in0=e_[0:16, 0:16], in1=En[0:16, gh, :], op=ALU.mult),
                              [eB_, EnB], [pB_])
                        nP, nPB = rN.next()
                        dP, dPB = rD.next()
                        mm(nP[:, 0:16], vnT[0:16, gh, :], p_[0:16, 0:16], True, True, [vnTB, pB_], [nPB], signal=False)
                        mm(dP[:, 0:16], ones_bf[0:16, :], p_[0:16, 0:16], True, True, [onesB, pB_], [dPB], signal=True)
                        accum(hh, 0, 16, nP, nPB, dP, dPB)
                        first[hh] = False
                li = 0
                for b in range(4):
                    for g_ in range(3):
                        Lg, dd = GROUPS[g_]
                        ni = 1 if g_ == 0 else 4
                        kc_, kcB_, vc_, vcB_, kt_, ktB_ = Kc[li % 2], KcB[li % 2], Vc[li % 2], VcB[li % 2], KcT[li % 2], KcTB[li % 2]
                        li += 1
                        for (dst, dB, srcd) in ((kc_, kcB_, ck_d[g_]), (vc_, vcB_, cv_d[g_])):
                            if g_ == 0:
                                srcap = srcd[b, :, :].rearrange("(n i) c -> n i c", i=1)
                            else:
                                srcap = srcd[b, :, :].rearrange("(n s) c -> n s c", s=dd)[:, 0:4, :]
                            kb.dma(pool, dst[:, 0:ni, :], srcap, reads=[], writes=[dB], sem_buf=dB)
                        for i in range(ni):
                            hb = i % 2
                            for hh in range(4):
                                tr(PB[:, hb * 512 + hh * 128: hb * 512 + (hh + 1) * 128], kc_[:, i, hh * 128:(hh + 1) * 128], identb[:, :], [kcB_, identbB], [PBb[hb]],
                                   signal=(hh == 3))
                            kb.op(act, lambda i=i, hb=hb, kt_=kt_: S.copy(out=kt_[:, i, :, :], in_=PB[:, hb * 512:hb * 512 + 512].rearrange("p (a b) -> p a b", a=4)),
                                  [PBb[hb], ktB_], [ktB_])
                        for hh in range(4):
                            gh = 4 * g_ + hh
                            bank, bankB = rS.next()
                            if g_ == 0:
                                mm(bank[:, 0:4], kt_[:, 0, hh, :], qkvn[:, gh, 4 * b:4 * b + 4], True, True, [ktB_, qkvnB], [bankB], signal=True)
                            else:
                                for i in range(4):
                                    mm(bank[:, i:i + 1], kt_[:, i, hh, :], qkvn[:, gh, 4 * b + i:4 * b + i + 1], True, True, [ktB_, qkvnB], [bankB], signal=(i == 3))
                            e_, eB_, p_, pB_ = exs[cnt % 2], exsB[cnt % 2], pTs[cnt % 2], pTsB[cnt % 2]
                            cnt += 1
                            kb.op(act, lambda e_=e_, bank=bank: S.activation(out=e_[:, 0:4], in_=bank[:, 0:4], func=AF.Exp), [bankB], [eB_])
                            if g_ == 0:
                                kb.op(dve, lambda e_=e_, p_=p_, gh=gh: V.tensor_tensor(out=p_[:, 0:4], in0=e_[:, 0:4], in1=Esm[:, gh, 0:4], op=ALU.mult),
                                      [eB_, EsmB], [pB_])
                            else:
                                kb.op(dve, lambda e_=e_, p_=p_, gh=gh: V.tensor_scalar(out=p_[:, 0:4], in0=e_[:, 0:4], scalar1=Esm[:, gh, 0:1], scalar2=0.0,
                                                                                     op0=ALU.mult, op1=ALU.add), [eB_, EsmB], [pB_])
                            nP, nPB = rN.next()
                            dP, dPB = rD.next()
                            if g_ == 0:
                                mm(nP[:, 0:4], vc_[:, 0, hh * 128:(hh + 1) * 128], p_[:, 0:4], True, True, [vcB_, pB_], [nPB], signal=False)
                            else:
                                for i in range(4):
                                    mm(nP[:, i:i + 1], vc_[:, i, hh * 128:(hh + 1) * 128], p_[:, i:i + 1], True, True, [vcB_, pB_], [nPB], signal=False)
                            mm(dP[:, 0:4], ones_bf[:, :], p_[:, 0:4], True, True, [onesB, pB_], [dPB], signal=True)
                            accum(hh, 4 * b, 4, nP, nPB, dP, dPB)
                kb.op(dve, lambda: V.reciprocal(out=aD[:, :, :], in_=aD[:, :, :]), [aDB], [aDB])
                kb.op(dve, lambda: V.tensor_tensor(out=att[:, :, TPH:T], in0=aN[:, :, :], in1=aD[:, :, :], op=ALU.mult), [aNB, aDB] + attB, attB)
                kb.barrier()

            if MIX_STOP <= 4:
                return False
            with ExitStack() as pe_:
                h = psb("mxh2", [128, NFT, T], BF16, pe_)
                hB = [Buf(f"mh2_{i}") for i in range(NFT)]
                hsem2 = Buf("hsem2")
                for i in range(NFT):
                    kb.dma(sp, h[:, i, :], HD[i], reads=[HDb[i]], writes=[hB[i]], sem_buf=hsem2)
                kb.group_total(hsem2, HDb + hB)
                sg = [psb(f"esg{i}", [128, 512], F32, pe_) for i in range(2)]
                sgB = [Buf(f"esg{i}") for i in range(2)]
                sink = FSink(pe_)
                specs = []
                for half in range(2):
                    specs.append(wspec(w_ao, 1024 * half, 1024))
                    for q in range(4):
                        specs.append(wspec(w_in, W_IN_OFF["ga"] + 1024 * half + 256 * q, 256))
                for c in range(8):
                    specs.append(wspec(w_out, 256 * c, 256))
                st = WStream(ring, specs, max_la=3)
                rot = Rot([0, 1, 2, 3])
                k = 0
                for half in range(2):
                    vao, bao, _ = st.get(5 * half)
                    for q in range(4):
                        vg, bg_, _ = st.get(5 * half + 1 + q)
                        for jj in range(2):
                            mt = 8 * half + 2 * q + jj
                            for (c0, n) in COLT:
                                bA, bAB = rot.next()
                                bG, bGB = rot.next()
                                for kt in range(4):
                                    mm(bA[:, 0:n], vao[:, kt, (2 * q + jj) * 128:(2 * q + jj + 1) * 128], att[:, kt, c0:c0 + n], kt == 0, kt == 3,
                                       bao + [attB[kt]], [bAB], signal=(kt == 3))
                                for kt in range(16):
                                    mm(bG[:, 0:n], vg[:, kt, jj * 128:(jj + 1) * 128], h[:, kt, c0:c0 + n], kt == 0, kt == 15,
                                       bg_ + [hB[kt]], [bGB], signal=(kt == 15))
                                s_, sB_ = sg[k % 2], sgB[k % 2]
                                k += 1
                                kb.op(act, lambda s_=s_, bG=bG, n=n: S.activation(out=s_[:, 0:n], in_=bG[:, 0:n], func=AF.Sigmoid), [bGB], [sB_])
                                kb.op(dve, lambda s_=s_, bA=bA, n=n: V.tensor_tensor(out=s_[:, 0:n], in0=s_[:, 0:n], in1=bA[:, 0:n], op=ALU.mult), [sB_, bAB], [sB_])
                                kb.op(dve, lambda s_=s_, mt=mt, c0=c0, n=n: V.tensor_tensor(out=m1[:, mt, c0:c0 + n], in0=s_[:, 0:n], in1=m1[:, mt, c0:c0 + n], op=ALU.add),
                                      [sB_, m1B[mt]], [m1B[mt]])
                        st.done(5 * half + 1 + q)
                    st.done(5 * half)
                for c in range(8):
                    vw, bw, _ = st.get(10 + c)
                    for jj in range(2):
                        m = 2 * c + jj
                        f_, fB_ = sink.buf(m)
                        for (c0, n) in COLT:
                            bank, bankB = rot.next()
                            for kt in range(16):
                                mm(bank[:, 0:n], vw[:, kt, jj * 128:(jj + 1) * 128], m1[:, kt, c0:c0 + n], kt == 0, kt == 15,
                                   bw + [m1B[kt]], [bankB], signal=(kt == 15))
                            kb.op(act, lambda f_=f_, bank=bank, c0=c0, n=n: S.copy(out=f_[:, c0:c0 + n], in_=bank[:, 0:n]), [bankB, fB_], [fB_])
                        sink.finish_tile(m)
                    st.done(10 + c)
                sink.flush()
                kb.barrier()
        epilogue(dump=dbg.get("x2"))
        return True

    ONLY_MD_G = globals().get("ONLY_MD", False)
    for c in range(0 if ONLY_MD_G else 24):
        ada_chunk(c)
    for gi in range(3):
        L = GROUPS[gi][0]
        for b in range(4):
            for r0 in range(0, L - 4, 512):
                nr = min(512, L - 4 - r0)
                d2d_jobs.append((sk_o[gi][b, r0:r0 + nr, :], ck_d[gi][b, 4 + r0:4 + r0 + nr, :]))
                d2d_jobs.append((sv_o[gi][b, r0:r0 + nr, :], cv_d[gi][b, 4 + r0:4 + r0 + nr, :]))
    for b in range(4):
        d2d_jobs.append((sconv_o[b, 0:26, :], state_d[b, 4:30, :]))

    def ada_inter(c):
        for cc in range(24 + 3 * c, min(24 + 3 * c + 3, 72)):
            ada_chunk(cc)

    if not ONLY_MD_G:
        ffn(0, w1[0], w3[0], w2[0], interleave=ada_inter)
        assert ada_stream.nxt == 72
    if DEBUG:
        dbg["x1"] = dout("dbg_x1", [NFT, 128, T])
        dbg["M"] = dout("dbg_M", [128, 720])
        kb.dma(sp, dbg["M"], Mt[:, :, :].rearrange("p a b -> p (a b)"), reads=[MB], writes=[outB], sem_buf=MB, partial=True)
    if not ONLY_MD_G:
        epilogue(dump=dbg.get("x1"))

    STOP = globals().get("STOP_AFTER", 99)
    mix_ok = True
    if STOP >= 2:
        if DEBUG:
            dbg["x2"] = dout("dbg_x2", [NFT, 128, T])
        mix_ok = mixer()
    if STOP >= 3 and mix_ok:
        ffn(2, w1[1], w3[1], w2[1])
        epilogue(final=True)
    while d2d_jobs:
        o_, i_ = d2d_jobs.pop(0)
        kb.dma(sp, o_, i_, reads=[], writes=[outB], sem_buf=d2dB, partial=True)
    kb.barrier()
    es.close()
    _LAST['n_ins'] = kb.n_ins
    _LAST['nsem'] = kb.nsem
    return nc


def _t5_buckets(dist):
    n = np.asarray(dist)
    max_exact = 16
    large = max_exact + (np.log(np.maximum(n, 1) / max_exact) / np.log(2048 / max_exact) * (32 - max_exact)).astype(np.int64)
    large = np.minimum(large, 31)
    return np.where(n < max_exact, n, large).astype(np.int32)


def _fm(v, nt):
    return np.ascontiguousarray(np.asarray(v, np.float32).reshape(nt, 128).T)


_LAST = {}


def kernel(x_prompt, x_sample, cache_k_w128, cache_v_w128, cache_k_w512, cache_v_w512,
           cache_k_w2048, cache_v_w2048, state_conv, c_prompt, c_sample,
           w_ada, b_ada, ffn1_norm_pre, ffn1_norm_post, ffn1_w1, ffn1_w3, ffn1_w2,
           mix_norm_pre, mix_norm_post, w_in, dw_kernel, dw_bias, conv_ln_g, conv_ln_b,
           w_conv_out, w_att_out, w_out, rel_bias,
           ffn2_norm_pre, ffn2_norm_post, ffn2_w1, ffn2_w3, ffn2_w2):
    f32 = np.float32
    A = lambda a: np.ascontiguousarray(np.asarray(a, f32))
    nc = build_program()

    ident = np.eye(128, dtype=f32)
    sel = np.zeros((32, 3 * 129), f32)
    for g, (w, d) in enumerate(GROUPS):
        bk = _t5_buckets(d * np.arange(129))
        sel[bk, g * 129 + np.arange(129)] = 1.0
    rowm = np.zeros((12, 3), f32)
    for r in range(12):
        rowm[r, r // 4] = 1.0
    masks = np.zeros((16, 32), f32)
    for p in range(16):
        for q in range(16):
            if p // 4 == q // 4:
                masks[p, q] = 1.0
        masks[p, 16 + p] = 1.0

    vecs = np.zeros((128, NVEC), f32)
    vecs[:, V_BADA:V_BADA + 144] = _fm(b_ada[0], 144)
    for i, nv in enumerate((ffn1_norm_pre, ffn1_norm_post, mix_norm_pre, mix_norm_post, ffn2_norm_pre, ffn2_norm_post)):
        vecs[:, V_NORM + 16 * i:V_NORM + 16 * i + 16] = _fm(nv[0], 16)
    vecs[:, V_DWK:V_DWK + 248] = np.asarray(dw_kernel[0], f32).reshape(31, 8, 128).transpose(2, 1, 0).reshape(128, 248)
    vecs[:, V_DWB:V_DWB + 8] = _fm(dw_bias[0], 8)
    vecs[:, V_LNG:V_LNG + 8] = _fm(conv_ln_g[0], 8)
    vecs[:, V_LNB:V_LNB + 8] = _fm(conv_ln_b[0], 8)

    xp = np.asarray(x_prompt, f32)[0]
    xs = np.asarray(x_sample, f32)
    cks = [np.asarray(c, f32)[0] for c in (cache_k_w128, cache_k_w512, cache_k_w2048)]
    cvs = [np.asarray(c, f32)[0] for c in (cache_v_w128, cache_v_w512, cache_v_w2048)]
    st = np.asarray(state_conv, f32)[0]
    shared = dict(
        vecs=vecs, ident=ident, sel=sel, relb=A(rel_bias), masks=masks, rowm=rowm,
        w_ada=A(w_ada[0]), w1a=A(ffn1_w1[0]), w3a=A(ffn1_w3[0]), w2a=A(ffn1_w2[0]),
        w_in=A(w_in[0]), w_co=A(w_conv_out[0]), w_ao=A(w_att_out[0]), w_out=A(w_out[0]),
        w1b=A(ffn2_w1[0]), w3b=A(ffn2_w3[0]), w2b=A(ffn2_w2[0]),
    )
    in_maps = []
    for c in range(NCORES):
        xin = np.zeros((T, D), f32)
        xin[0:TP] = xp[TP * c:TP * (c + 1)]
        if c > 0:
            xin[TP:TPH] = xp[TP * c - NHALO:TP * c]
        xin[TPH:T] = xs[4 * c:4 * c + 4].reshape(NS, D)
        c_all = np.concatenate([np.asarray(c_prompt, f32)[0:1], np.asarray(c_sample, f32)[4 * c:4 * c + 4]], 0)
        cT = np.ascontiguousarray(c_all.T.reshape(16, 128, 5).transpose(1, 0, 2).reshape(128, 80))
        flags = np.zeros((128, 4), f32)
        flags[:, 0] = 1.0 if c >= 1 else 0.0
        flags[:, 1] = 1.0 if c >= 2 else 0.0
        flags[:, 2] = 1.0 if c >= 1 else 0.0
        m = dict(shared)
        m.update(xin=xin, cT=cT, flags=flags, state=A(st[4 * c:4 * c + 4]))
        for g in range(3):
            L = GROUPS[g][0]
            m[f"ck{g}"] = A(cks[g][4 * c:4 * c + 4].reshape(4, L, 512))
            m[f"cv{g}"] = A(cvs[g][4 * c:4 * c + 4].reshape(4, L, 512))
        in_maps.append(m)

    res = run_bass_kernel_spmd(nc, in_maps, core_ids=list(range(NCORES)))
    R = res.results
    _LAST["res"] = R
    y_prompt = np.concatenate([R[c]["y"][0:TP] for c in range(NCORES)], 0)[None]
    y_sample = np.concatenate([R[c]["y"][TP:TP + NS].reshape(4, 4, D) for c in range(NCORES)], 0)
    outs = [y_prompt, y_sample]
    for g, (w, d) in enumerate(GROUPS):
        for nm in ("kout", "vout"):
            if w <= TP:
                rows = R[7][nm][g][TP - w:TP]
            else:
                rows = np.concatenate([R[6][nm][g], R[7][nm][g]], 0)
            outs.append(np.ascontiguousarray(rows.reshape(1, 1, w, 4, 128)))
    outs.append(np.ascontiguousarray(R[7]["pconv"].reshape(1, 1, NHALO, CONV_CH)))
    for g, (w, d) in enumerate(GROUPS):
        for nm in ("sk", "sv"):
            outs.append(np.concatenate([R[c][f"{nm}{g}"] for c in range(NCORES)], 0).reshape(1, 32, w, 4, 128))
    outs.append(np.concatenate([R[c]["sconv"] for c in range(NCORES)], 0).reshape(1, 32, 30, CONV_CH))
    return tuple(np.ascontiguousarray(o, dtype=f32) for o in outs)
```

```python
import numpy as np
from contextlib import ExitStack
import concourse.bass as bass
import concourse.mybir as mybir
from concourse.bass_utils import run_bass_kernel_spmd

F32 = mybir.dt.float32
BF16 = mybir.dt.bfloat16
AF = mybir.ActivationFunctionType
ALU = mybir.AluOpType

NCORES = 8
D = 2048
NFT = 16
DFF = 5632
NFF = 44
TP = 1024
NHALO = 30
NS = 16
TPH = TP + NHALO
T = TPH + NS
COLT = [(0, 512), (512, 512), (1024, 46)]
CONV_CH = 1024
EPS = 1e-6
GROUPS = ((128, 1), (512, 4), (2048, 16))
HG = (128, 512, 1024)
SEND_BASE = (0, 512, 2560)
SEND_HALF = 6656
SEND_W = 2 * SEND_HALF
W_IN_OFF = dict(a=0, b=1024, q=2048, k=3584, v=5120, gc=6656, ga=8704)
V_BADA = 0
V_NORM = 144
V_DWK = 240
V_DWB = 488
V_LNG = 496
V_LNB = 504
NVEC = 512

DEBUG = False


class Buf:
    __slots__ = ("name", "w", "r", "slot", "pend")

    def __init__(self, name):
        self.name = name
        self.w = {}
        self.r = {}
        self.slot = None
        self.pend = None


class Eng:
    def __init__(self, kb, name, handle):
        self.name = name
        self.h = handle
        self.sem = kb.new_sem("e_" + name)
        self.seq = 0
        self.waited = {}
        self.pending = []


class KB:
    def __init__(self, nc):
        self.nc = nc
        self.es = ExitStack()
        self.nsem = 0
        self.slots = []
        self.free = {}
        self.holders = []
        self.pe = Eng(self, "pe", nc.tensor)
        self.act = Eng(self, "act", nc.scalar)
        self.dve = Eng(self, "dve", nc.vector)
        self.pool = Eng(self, "pool", nc.gpsimd)
        self.sp = Eng(self, "sp", nc.sync)
        self.engs = [self.pe, self.act, self.dve, self.pool, self.sp]
        self.n_ins = 0

    def new_sem(self, name):
        self.nsem += 1
        return self.es.enter_context(self.nc.semaphore(f"{name}_{self.nsem}"))

    def _wait(self, eng, ev):
        sem, val = ev
        k = id(sem)
        if eng.waited.get(k, 0) >= val:
            return
        if sem is eng.sem and eng is self.pe:
            return
        eng.h.wait_ge(sem, val)
        eng.waited[k] = val

    def _deps(self, eng, reads, writes, partial=False):
        for b in reads:
            assert b.pend is None or b.pend is eng, (b.name, "pending unsignaled access")
            for ev in b.w.values():
                self._wait(eng, ev)
        for b in writes:
            assert b.pend is None or b.pend is eng, (b.name, "pending unsignaled access")
            if not partial:
                for ev in b.w.values():
                    self._wait(eng, ev)
            for ev in b.r.values():
                self._wait(eng, ev)

    @staticmethod
    def _rec(ev, reads, writes, partial=False):
        k = id(ev[0])
        for b in writes:
            if partial:
                b.w[k] = ev
            else:
                b.w = {k: ev}
            b.r = {}
            b.pend = None
        for b in reads:
            if b in writes:
                continue
            b.r[k] = ev
            b.pend = None

    def op(self, eng, fn, reads=(), writes=(), signal=True):
        self._deps(eng, reads, writes)
        ins = fn()
        self.n_ins += 1
        if signal:
            eng.seq += 1
            ins.then_inc(eng.sem, 1)
            ev = (eng.sem, eng.seq)
            for (rs, ws) in eng.pending:
                self._rec(ev, rs, ws)
            eng.pending = []
            self._rec(ev, reads, writes)
        else:
            eng.pending.append((reads, writes))
            for b in list(reads) + list(writes):
                b.pend = eng
        return ins

    def _slot(self, sb, kind):
        if sb.slot is None:
            fl = self.free.setdefault(kind, [])
            if fl:
                sb.slot = fl.pop()
            else:
                sb.slot = [self.new_sem("d" + kind), 0, kind]
                self.slots.append(sb.slot)
            self.holders.append(sb)
        assert sb.slot[2] == kind, (sb.name, sb.slot[2], kind)
        return sb.slot

    def dma(self, q, out, in_, reads=(), writes=(), sem_buf=None, partial=False):
        self._deps(q, reads, writes, partial=partial)
        sl = self._slot(sem_buf, q.name)
        sl[1] += 16
        ins = q.h.dma_start(out=out, in_=in_).then_inc(sl[0], 16)
        self.n_ins += 1
        self._rec((sl[0], sl[1]), reads, writes, partial=partial)
        return ins

    def custom(self, eng, fn, reads, writes, sem_buf, inc):
        self._deps(eng, reads, writes)
        sl = self._slot(sem_buf, "cc")
        sl[1] += inc
        ins = fn().then_inc(sl[0], inc)
        self._rec((sl[0], sl[1]), reads, writes)
        return ins

    def group_total(self, sem_buf, bufs):
        sl = sem_buf.slot
        ev = (sl[0], sl[1])
        k = id(sl[0])
        for b in bufs:
            if k in b.w:
                b.w[k] = ev
            if k in b.r:
                b.r[k] = ev

    def barrier(self):
        for e in self.engs:
            assert not e.pending, e.name
        evs = [(e.sem, e.seq) for e in self.engs if e.seq > 0]
        evs += [(sl[0], sl[1]) for sl in self.slots if sl[1] > 0]
        for e in self.engs:
            for ev in evs:
                if ev[0] is e.sem:
                    continue
                self._wait(e, ev)
        for b in self.holders:
            self.free[b.slot[2]].append(b.slot)
            b.slot = None
        self.holders = []


class Ring:
    SLOT = 4096

    def __init__(self, cx, nslots=6):
        self.cx = cx
        self.n = nslots
        self.t = cx.gsb("wring", [128, nslots * self.SLOT], BF16)
        self.bufs = [Buf(f"ring{i}") for i in range(nslots)]
        self.busy = [False] * nslots
        self.pos = 0

    def try_load(self, src3, kt, ncols, nsl):
        cands = [p for p in range(0, self.n - nsl + 1) if p % nsl == 0]
        cands = [p for p in cands if p >= self.pos] + [p for p in cands if p < self.pos]
        pos = None
        for p in cands:
            if not any(self.busy[s] for s in range(p, p + nsl)):
                pos = p
                break
        if pos is None:
            return None
        sl = list(range(pos, pos + nsl))
        for s in sl:
            self.busy[s] = True
        self.pos = (pos + nsl) % self.n
        assert kt * ncols <= nsl * self.SLOT
        view = self.t[:, pos * self.SLOT: pos * self.SLOT + kt * ncols].rearrange("p (k n) -> p k n", k=kt)
        bufs = [self.bufs[s] for s in sl]
        kb = self.cx.kb
        kb.dma(kb.pool, view, src3, reads=[], writes=bufs, sem_buf=bufs[0])
        return (view, bufs, sl)

    def release(self, h):
        for s in h[2]:
            self.busy[s] = False


class WStream:
    def __init__(self, ring, specs, max_la=4):
        self.ring = ring
        self.specs = specs
        self.h = [None] * len(specs)
        self.nxt = 0
        self.max_la = max_la

    def get(self, i):
        while self.nxt < len(self.specs) and self.nxt <= i + self.max_la:
            h = self.ring.try_load(*self.specs[self.nxt])
            if h is None:
                break
            self.h[self.nxt] = h
            self.nxt += 1
        assert self.h[i] is not None, ("ring deadlock", i)
        return self.h[i]

    def done(self, i):
        self.ring.release(self.h[i])


def wspec(w, c0, ncols, nsl=1):
    K = w.shape[0]
    kt = K // 128
    return (w[:, c0:c0 + ncols].rearrange("(k p) n -> p k n", p=128), kt, ncols, nsl)


class Cx:
    pass


def build_program():
    nc = bass.Bass("TRN2", target_bir_lowering=False)
    cx = Cx()
    cx.nc = nc
    kb = cx.kb = KB(nc)
    es = kb.es
    pe, act, dve, pool, sp = kb.pe, kb.act, kb.dve, kb.pool, kb.sp
    V, S, G, PEn = nc.vector, nc.scalar, nc.gpsimd, nc.tensor

    _cnt = [0]

    def SBT(name, shape, dt):
        _cnt[0] += 1
        return nc.sbuf_tensor(f"{name}_u{_cnt[0]}", shape, dt)

    def din(name, shape, dt=F32):
        return nc.dram_tensor(name, list(shape), dt, kind="ExternalInput").ap()

    def dout(name, shape, dt=F32):
        return nc.dram_tensor(name, list(shape), dt, kind="ExternalOutput").ap()

    def dscr(name, shape, dt=F32):
        return nc.dram_tensor(name, list(shape), dt, kind="Internal").ap()

    def gsb(name, shape, dt):
        return es.enter_context(SBT("g_" + name, list(shape), dt))
    cx.gsb = gsb

    xin = din("xin", [T, D])
    cT_d = din("cT", [128, 80])
    vecs_d = din("vecs", [128, NVEC])
    ident_d = din("ident", [128, 128])
    sel_d = din("sel", [32, 3 * 129])
    relb_d = din("relb", [32, 12])
    flags_d = din("flags", [128, 4])
    masks_d = din("masks", [16, 32])
    ck_d = [din(f"ck{g}", [4, GROUPS[g][0], 512]) for g in range(3)]
    cv_d = [din(f"cv{g}", [4, GROUPS[g][0], 512]) for g in range(3)]
    state_d = din("state", [4, 30, CONV_CH])
    w_ada = din("w_ada", [D, 9 * D])
    w1 = [din("w1a", [D, DFF]), din("w1b", [D, DFF])]
    w3 = [din("w3a", [D, DFF]), din("w3b", [D, DFF])]
    w2 = [din("w2a", [DFF, D]), din("w2b", [DFF, D])]
    w_in = din("w_in", [D, 10752])
    w_co = din("w_co", [CONV_CH, D])
    w_ao = din("w_ao", [512, D])
    w_out = din("w_out", [D, D])

    y_o = dout("y", [TP + NS, D])
    kout = dout("kout", [3, TP, 512])
    vout = dout("vout", [3, TP, 512])
    pconv_o = dout("pconv", [NHALO, CONV_CH])
    sk_o = [dout(f"sk{g}", [4, GROUPS[g][0], 512]) for g in range(3)]
    sv_o = [dout(f"sv{g}", [4, GROUPS[g][0], 512]) for g in range(3)]
    sconv_o = dout("sconv", [4, 30, CONV_CH])

    XT = dscr("XT", [NFT, 128, T])
    FD = dscr("FD", [NFT, 128, T])
    QKV = dscr("QKV", [36, 128, T], BF16)
    send = dscr("send", [128, SEND_W], BF16)
    gath = dscr("gath", [NCORES * 128, SEND_W], BF16)
    big_d = nc.dram_tensor("bigE", [12, 130, 384], F32, kind="Internal")
    rowm_d = din("rowm", [12, 3])
    HD = dscr("HD", [NFT, 128, T], BF16)
    HDb = [Buf(f"HD{i}") for i in range(NFT)]
    XTb = [Buf(f"XT{i}") for i in range(NFT)]
    FDb = [Buf(f"FD{i}") for i in range(NFT)]
    QKVb = [Buf(f"QKV{i}") for i in range(36)]
    halo1 = dscr("halo1", [128, SEND_W], BF16)
    halo2 = dscr("halo2", [128, SEND_W], BF16)
    halo1B = Buf("halo1")
    halo2B = Buf("halo2")
    sendB = Buf("send")
    gathB = Buf("gath")
    bigB = Buf("bigE")
    outB = Buf("outs")
    dbg = {}

    PS = [es.enter_context(nc.psum_tensor(f"ps{i}", [128, 512], F32)) for i in range(7)]
    PSb = [Buf(f"ps{i}") for i in range(7)]
    PB = es.enter_context(nc.psum_tensor("psb", [128, 1024], BF16))
    _pbb = Buf("psb")
    PBb = [_pbb, _pbb]

    class Rot:
        def __init__(self, idx):
            self.idx = idx
            self.i = 0

        def next(self):
            j = self.idx[self.i % len(self.idx)]
            self.i += 1
            return PS[j], PSb[j]

    ident = gsb("ident", [128, 128], F32); identB = Buf("ident")
    identb = gsb("identb", [128, 128], BF16); identbB = Buf("identb")
    ones_bf = gsb("ones_bf", [128, 128], BF16); onesB = Buf("ones")
    vecs = gsb("vecs", [128, NVEC], F32); vecsB = Buf("vecs")
    Mt = gsb("Mt", [128, 144, 5], F32); MB = Buf("M")
    sc = gsb("sc", [128, 80], BF16); scB = Buf("sc")
    flags = gsb("flags", [128, 4], F32); flagsB = Buf("flags")
    epsT = gsb("epsT", [128, 1], F32); epsB = Buf("eps")
    rstd = gsb("rstd", [128, T], F32); rstdB = Buf("rstd")
    ring = Ring(cx, 6)
    cx.ring = ring

    def mm(out, lhsT, rhs, start, stop, reads, writes, signal):
        return kb.op(pe, lambda: PEn.matmul(out, lhsT=lhsT, rhs=rhs, start=start, stop=stop), reads, writes, signal=signal)

    def tr(out, in_, idt, reads, writes, signal=True):
        return kb.op(pe, lambda: PEn.transpose(out=out, in_=in_, identity=idt), reads, writes, signal=signal)

    kb.dma(sp, ident[:], ident_d, writes=[identB], sem_buf=identB)
    kb.dma(sp, vecs[:], vecs_d, writes=[vecsB], sem_buf=vecsB)
    kb.dma(sp, flags[:], flags_d, writes=[flagsB], sem_buf=flagsB)
    kb.op(dve, lambda: V.tensor_copy(out=identb[:], in_=ident[:]), [identB], [identbB])
    kb.op(dve, lambda: V.memset(ones_bf[:], 1.0), [], [onesB])
    kb.op(dve, lambda: V.memset(epsT[:], EPS), [], [epsB])

    def stats_to_rstd(nfeat):
        for ci, (c0, n) in enumerate(COLT):
            kb.op(act, lambda ci=ci, c0=c0, n=n: S.activation(out=rstd[:, c0:c0 + n], in_=PS[4 + ci][:, 0:n], func=AF.Sqrt,
                                                             bias=epsT[:, 0:1], scale=1.0 / nfeat),
                  [PSb[4 + ci], epsB], [rstdB])
        kb.op(dve, lambda: V.reciprocal(out=rstd[:], in_=rstd[:]), [rstdB], [rstdB])

    def stats_accum(sq_ap, sqB, i, n_tiles):
        for ci, (c0, n) in enumerate(COLT):
            mm(PS[4 + ci][:, 0:n], ones_bf[:], sq_ap[:, c0:c0 + n], i == 0, i == n_tiles - 1,
               [onesB, sqB], [PSb[4 + ci]], signal=(i == n_tiles - 1 or ci == 2))

    with ExitStack() as ph:
        def psb(name, shape, dt):
            return ph.enter_context(SBT("p_" + name, list(shape), dt))
        cTt = psb("cTt", [128, 80], F32); cTB = Buf("cT")
        kb.dma(sp, cTt[:], cT_d, writes=[cTB], sem_buf=cTB)
        kb.op(act, lambda: S.activation(out=sc[:], in_=cTt[:], func=AF.Silu), [cTB], [scB])

        relb = psb("relb", [32, 12], F32); relbB = Buf("relb")
        sel = psb("sel", [32, 3 * 129], F32); selB = Buf("sel")
        bv = psb("bv", [12, 384], F32); bvB = Buf("bv")
        bt = psb("bt", [12, 3 * 129], F32); btB = Buf("bt")
        rowm = psb("rowm", [12, 3], F32); rowmB = Buf("rowm")
        kb.dma(sp, relb[:], relb_d, writes=[relbB], sem_buf=relbB)
        kb.dma(sp, sel[:], sel_d, writes=[selB], sem_buf=selB)
        kb.dma(sp, rowm[:], rowm_d, writes=[rowmB], sem_buf=rowmB)
        kb.op(dve, lambda: V.memset(bv[:], 0.0), [], [bvB])
        mm(PS[0][0:12, 0:387], relb[:, :], sel[:, :], True, True, [relbB, selB], [PSb[0]], signal=True)
        kb.op(act, lambda: S.activation(out=bt[:, :], in_=PS[0][0:12, 0:387], func=AF.Exp), [PSb[0]], [btB])
        kb.op(dve, lambda: V.tensor_scalar(out=bv[:, 0:129], in0=bt[:, 0:129], scalar1=rowm[:, 0:1], scalar2=0.0,
                                           op0=ALU.mult, op1=ALU.add), [btB, rowmB, bvB], [bvB])
        for g in (1, 2):
            kb.op(dve, lambda g=g: V.scalar_tensor_tensor(out=bv[:, 0:129], in0=bt[:, g * 129:(g + 1) * 129],
                                                          scalar=rowm[:, g:g + 1], in1=bv[:, 0:129],
                                                          op0=ALU.mult, op1=ALU.add), [btB, rowmB, bvB], [bvB])
        kb.dma(sp, big_d.ap(), bv[:, :].unsqueeze(1).to_broadcast([12, 130, 384]), reads=[bvB], writes=[bigB], sem_buf=bvB)

        xT = psb("xT", [128, NFT, T], F32); xTB = Buf("xTall")
        xtok = [psb(f"xtok{i}", [128, D], F32) for i in range(3)]
        xtokB = [Buf(f"xtok{i}") for i in range(3)]
        sq0 = [psb(f"sq0_{i}", [128, T], BF16) for i in range(2)]
        sq0B = [Buf(f"sq0_{i}") for i in range(2)]
        rot = Rot([0, 1, 2, 3])
        ev = 0
        for tt in range(9):
            r0 = tt * 128
            nr = min(128, T - r0)
            xb_, xbB = xtok[tt % 3], xtokB[tt % 3]
            kb.dma(sp, xb_[0:nr, :], xin[r0:r0 + nr, :], writes=[xbB], sem_buf=xbB)
            for i0 in range(0, NFT, 4):
                bank, bankB = rot.next()
                for ii in range(4):
                    i = i0 + ii
                    tr(bank[:, ii * 128: ii * 128 + nr], xb_[0:nr, i * 128:(i + 1) * 128], ident[0:nr, 0:nr],
                       [xbB, identB], [bankB], signal=(ii == 3))
                src = bank[:, :].rearrange("p (a b) -> p a b", a=4)[:, :, 0:nr]
                dst = xT[:, i0:i0 + 4, r0:r0 + nr]
                if ev % 2 == 0:
                    kb.op(act, lambda src=src, dst=dst: S.copy(out=dst, in_=src), [bankB], [xTB])
                else:
                    kb.op(dve, lambda src=src, dst=dst: V.tensor_copy(out=dst, in_=src), [bankB], [xTB])
                ev += 1
        for i in range(NFT):
            kb.dma(sp, XT[i], xT[:, i, :], reads=[xTB], writes=[XTb[i]], sem_buf=xTB)
            sq_, sqB_ = sq0[i % 2], sq0B[i % 2]
            kb.op(act, lambda i=i, sq_=sq_: S.activation(out=sq_[:], in_=xT[:, i, :], func=AF.Square), [xTB], [sqB_])
            stats_accum(sq_, sqB_, i, NFT)
        kb.group_total(xTB, XTb + [xTB])
        stats_to_rstd(D)
        kb.barrier()

    ada_specs = [wspec(w_ada, 256 * c, 256) for c in range(72)]
    ada_stream = WStream(ring, ada_specs, max_la=1)
    ada_rot = Rot([6])

    def ada_chunk(c):
        view, wb, _ = ada_stream.get(c)
        bank, bankB = ada_rot.next()
        for jj in range(2):
            for kt in range(16):
                mm(bank[:, jj * 5:(jj + 1) * 5], view[:, kt, jj * 128:(jj + 1) * 128], sc[:, kt * 5:(kt + 1) * 5],
                   kt == 0, kt == 15, wb + [scB], [bankB], signal=(kt == 15))
        ada_stream.done(c)
        j0 = 2 * c
        kb.op(dve, lambda: V.tensor_tensor(out=Mt[:, j0:j0 + 2, :], in0=bank[:, 0:10].rearrange("p (a b) -> p a b", a=2),
                                           in1=vecs[:, V_BADA + j0:V_BADA + j0 + 2].unsqueeze(2).to_broadcast([128, 2, 5]),
                                           op=ALU.add), [bankB, vecsB], [MB])

    mods_t = gsb("mods", [128, 3, NFT], F32); modsB = Buf("mods")
    modS_t = gsb("modS", [128, 3, NFT, NS], F32); modSB = Buf("modS")
    mtmp = gsb("mtmp", [128, NFT, 4], F32); mtmpB = Buf("mtmp")

    def build_mods(sub, coef):
        jsh, jsc, jgt = (3 * sub) * 16, (3 * sub + 1) * 16, (3 * sub + 2) * 16
        gpre = vecs[:, V_NORM + (2 * sub) * 16: V_NORM + (2 * sub) * 16 + 16]
        gpost = vecs[:, V_NORM + (2 * sub + 1) * 16: V_NORM + (2 * sub + 1) * 16 + 16]
        R_ = [MB, vecsB]
        kb.op(dve, lambda: V.scalar_tensor_tensor(out=mods_t[:, 0, :], in0=Mt[:, jsc:jsc + 16, 0], scalar=1.0, in1=gpre,
                                                  op0=ALU.add, op1=ALU.mult), R_, [modsB])
        kb.op(dve, lambda: V.tensor_copy(out=mods_t[:, 1, :], in_=Mt[:, jsh:jsh + 16, 0]), R_ + [modsB], [modsB])
        kb.op(dve, lambda: V.scalar_tensor_tensor(out=mods_t[:, 2, :], in0=Mt[:, jgt:jgt + 16, 0], scalar=coef, in1=gpost,
                                                  op0=ALU.mult, op1=ALU.mult), R_ + [modsB], [modsB])
        def expand(k):
            kb.op(dve, lambda: V.tensor_copy(out=modS_t[:, k, :, :].rearrange("p t (b i) -> p t b i", b=4),
                                             in_=mtmp[:, :, :].unsqueeze(3).to_broadcast([128, NFT, 4, 4])),
                  [mtmpB, modSB], [modSB])
        kb.op(dve, lambda: V.scalar_tensor_tensor(out=mtmp[:, :, :], in0=Mt[:, jsc:jsc + 16, 1:5], scalar=1.0,
                                                  in1=gpre.unsqueeze(2).to_broadcast([128, NFT, 4]),
                                                  op0=ALU.add, op1=ALU.mult), R_ + [mtmpB], [mtmpB])
        expand(0)
        kb.op(dve, lambda: V.tensor_copy(out=mtmp[:, :, :], in_=Mt[:, jsh:jsh + 16, 1:5]), R_ + [mtmpB, modSB], [mtmpB])
        expand(1)
        kb.op(dve, lambda: V.scalar_tensor_tensor(out=mtmp[:, :, :], in0=Mt[:, jgt:jgt + 16, 1:5], scalar=coef,
                                                  in1=gpost.unsqueeze(2).to_broadcast([128, NFT, 4]),
                                                  op0=ALU.mult, op1=ALU.mult), R_ + [mtmpB, modSB], [mtmpB])
        expand(2)

    def prologue(ph, h, hB):
        xs = [ph.enter_context(SBT(f"pxs{i}", [128, T], F32)) for i in range(2)]
        xsB = [Buf(f"pxs{i}") for i in range(2)]
        for i in range(NFT):
            x_, xB_ = xs[i % 2], xsB[i % 2]
            kb.dma(sp, x_[:], XT[i], reads=[XTb[i]], writes=[xB_], sem_buf=xB_)
            kb.op(dve, lambda x_=x_: V.tensor_tensor(out=x_[:], in0=x_[:], in1=rstd[:], op=ALU.mult), [xB_, rstdB], [xB_])
            kb.op(act, lambda x_=x_, i=i: S.activation(out=h[:, i, 0:TPH], in_=x_[:, 0:TPH], func=AF.Identity,
                                                       bias=mods_t[:, 1, i:i + 1], scale=mods_t[:, 0, i:i + 1]),
                  [xB_, modsB], [hB[i]])
            kb.op(dve, lambda x_=x_, i=i: V.tensor_tensor(out=x_[:, TPH:T], in0=x_[:, TPH:T], in1=modS_t[:, 0, i, :], op=ALU.mult),
                  [xB_, modSB], [xB_])
            kb.op(dve, lambda x_=x_, i=i: V.tensor_tensor(out=h[:, i, TPH:T], in0=x_[:, TPH:T], in1=modS_t[:, 1, i, :], op=ALU.add),
                  [xB_, modSB, hB[i]], [hB[i]])

    class FSink:
        def __init__(self, ph):
            self.fst = [ph.enter_context(SBT(f"fst{i}", [128, T], F32)) for i in range(2)]
            self.fstB = [Buf(f"fst{i}") for i in range(2)]
            self.sq = [ph.enter_context(SBT(f"fsq{i}", [128, T], BF16)) for i in range(2)]
            self.sqB = [Buf(f"fsq{i}") for i in range(2)]
            self.deferred = None

        def buf(self, m):
            return self.fst[m % 2], self.fstB[m % 2]

        def finish_tile(self, m):
            f_, fB_ = self.buf(m)
            sq_, sqB_ = self.sq[m % 2], self.sqB[m % 2]
            kb.dma(sp, FD[m], f_[:], reads=[fB_], writes=[FDb[m]], sem_buf=fB_)
            kb.op(act, lambda: S.activation(out=sq_[:], in_=f_[:], func=AF.Square), [fB_], [sqB_])
            self.flush()
            self.deferred = (sq_, sqB_, m)

        def flush(self):
            if self.deferred is not None:
                sq_, sqB_, m = self.deferred
                stats_accum(sq_, sqB_, m, NFT)
                self.deferred = None

    def epilogue(final=False, dump=None):
        with ExitStack() as ph:
            stats_to_rstd(D)
            NB = 3
            xs = [ph.enter_context(SBT(f"exs{i}", [128, T], F32)) for i in range(NB)]
            xsB = [Buf(f"exs{i}") for i in range(NB)]
            fs = [ph.enter_context(SBT(f"efs{i}", [128, T], F32)) for i in range(NB)]
            fsB = [Buf(f"efs{i}") for i in range(NB)]

            def issue_loads(i):
                kb.dma(sp, xs[i % NB][:], XT[i], reads=[XTb[i]], writes=[xsB[i % NB]], sem_buf=xsB[i % NB])
                kb.dma(sp, fs[i % NB][:], FD[i], reads=[FDb[i]], writes=[fsB[i % NB]], sem_buf=fsB[i % NB])
            sq = [ph.enter_context(SBT(f"esq{i}", [128, T], BF16)) for i in range(2)]
            sqB = [Buf(f"esq{i}") for i in range(2)]
            if final:
                yst = ph.enter_context(SBT("yst", [128, 9, D], F32))
                ystB = Buf("yst")
                rot = Rot([0, 1, 2, 3])
            pend = None
            issue_loads(0)
            for i in range(NFT):
                x_, xB_, f_, fB_ = xs[i % NB], xsB[i % NB], fs[i % NB], fsB[i % NB]
                if i + 1 < NFT:
                    issue_loads(i + 1)
                kb.op(dve, lambda f_=f_, i=i: V.scalar_tensor_tensor(out=f_[:, 0:TPH], in0=f_[:, 0:TPH], scalar=mods_t[:, 2, i:i + 1],
                                                                    in1=rstd[:, 0:TPH], op0=ALU.mult, op1=ALU.mult),
                      [fB_, modsB, rstdB], [fB_])
                kb.op(dve, lambda f_=f_, i=i: V.tensor_tensor(out=f_[:, TPH:T], in0=f_[:, TPH:T], in1=modS_t[:, 2, i, :], op=ALU.mult),
                      [fB_, modSB], [fB_])
                kb.op(dve, lambda f_=f_: V.tensor_tensor(out=f_[:, TPH:T], in0=f_[:, TPH:T], in1=rstd[:, TPH:T], op=ALU.mult),
                      [fB_, rstdB], [fB_])
                kb.op(dve, lambda f_=f_, x_=x_: V.tensor_tensor(out=x_[:], in0=x_[:], in1=f_[:], op=ALU.add), [fB_, xB_], [xB_])
                if not final:
                    kb.dma(sp, XT[i], x_[:], reads=[xB_], writes=[XTb[i]], sem_buf=xB_)
                    if dump is not None:
                        kb.dma(sp, dump[i], x_[:], reads=[xB_], writes=[outB], sem_buf=xB_, partial=True)
                    sq_, sqB_ = sq[i % 2], sqB[i % 2]
                    kb.op(act, lambda sq_=sq_, x_=x_: S.activation(out=sq_[:], in_=x_[:], func=AF.Square), [xB_], [sqB_])
                    if pend is not None:
                        stats_accum(*pend)
                    pend = (sq_, sqB_, i, NFT)
                else:
                    for q0 in range(0, 9, 4):
                        tts = list(range(q0, min(q0 + 4, 9)))
                        bank, bankB = rot.next()
                        for jj, tt in enumerate(tts):
                            c0, n = (tt * 128, 128) if tt < 8 else (TPH, NS)
                            tr(bank[0:n, jj * 128:(jj + 1) * 128], x_[:, c0:c0 + n], ident[:, :], [xB_, identB], [bankB],
                               signal=(jj == len(tts) - 1))
                        for jj, tt in enumerate(tts):
                            n = 128 if tt < 8 else NS
                            eng = act if (tt % 2 == 0) else dve
                            if eng is act:
                                kb.op(act, lambda jj=jj, tt=tt, n=n, bank=bank: S.copy(out=yst[0:n, tt, i * 128:(i + 1) * 128],
                                                                                     in_=bank[0:n, jj * 128:(jj + 1) * 128]),
                                      [bankB], [ystB])
                            else:
                                kb.op(dve, lambda jj=jj, tt=tt, n=n, bank=bank: V.tensor_copy(out=yst[0:n, tt, i * 128:(i + 1) * 128],
                                                                                            in_=bank[0:n, jj * 128:(jj + 1) * 128]),
                                      [bankB], [ystB])
            if pend is not None:
                stats_accum(*pend)
            if final:
                kb.dma(sp, y_o[0:TP, :].rearrange("(t p) d -> p t d", p=128), yst[:, 0:8, :], reads=[ystB], writes=[outB],
                       sem_buf=ystB, partial=True)
                kb.dma(sp, y_o[TP:TP + NS, :], yst[0:NS, 8, :], reads=[ystB], writes=[outB], sem_buf=ystB, partial=True)
            else:
                stats_to_rstd(D)
            kb.barrier()

    d2d_jobs = []
    d2dB = Buf("d2d")

    def ffn(sub, w1d, w3d, w2d, interleave=None):
        with ExitStack() as ph:
            g = ph.enter_context(SBT("ffg", [128, NFF, T], BF16))
            gB = [Buf(f"g{i}") for i in range(NFF)]
            with ExitStack() as ph1:
                h = ph1.enter_context(SBT("ffh", [128, NFT, T], BF16))
                hB = [Buf(f"h{i}") for i in range(NFT)]
                sg = [ph1.enter_context(SBT(f"sg{i}", [128, 512], F32)) for i in range(2)]
                sgB = [Buf(f"sg{i}") for i in range(2)]
                build_mods(sub, 0.5)
                prologue(ph1, h, hB)
                specs = []
                for c in range(22):
                    specs.append(wspec(w1d, 256 * c, 256))
                    specs.append(wspec(w3d, 256 * c, 256))
                st = WStream(ring, specs, max_la=3)
                rot = Rot([0, 1, 2, 3, 4, 5])
                k = 0
                for c in range(22):
                    v1, b1, _ = st.get(2 * c)
                    v3, b3, _ = st.get(2 * c + 1)
                    for jj in range(2):
                        ff = 2 * c + jj
                        for (c0, n) in COLT:
                            bg, bgB = rot.next()
                            bu, buB = rot.next()
                            for kt in range(16):
                                mm(bg[:, 0:n], v1[:, kt, jj * 128:(jj + 1) * 128], h[:, kt, c0:c0 + n], kt == 0, kt == 15,
                                   b1 + [hB[kt]], [bgB], signal=(kt == 15))
                            for kt in range(16):
                                mm(bu[:, 0:n], v3[:, kt, jj * 128:(jj + 1) * 128], h[:, kt, c0:c0 + n], kt == 0, kt == 15,
                                   b3 + [hB[kt]], [buB], signal=(kt == 15))
                            s_, sB_ = sg[k % 2], sgB[k % 2]
                            k += 1
                            kb.op(act, lambda s_=s_, bg=bg, n=n: S.activation(out=s_[:, 0:n], in_=bg[:, 0:n], func=AF.Silu),
                                  [bgB], [sB_])
                            kb.op(dve, lambda s_=s_, bu=bu, n=n, ff=ff, c0=c0: V.tensor_tensor(out=g[:, ff, c0:c0 + n], in0=s_[:, 0:n],
                                                                                             in1=bu[:, 0:n], op=ALU.mult),
                                  [sB_, buB, gB[ff]], [gB[ff]])
                    st.done(2 * c)
                    st.done(2 * c + 1)
                    if interleave is not None:
                        interleave(c)
                    if d2d_jobs:
                        for _ in range(2):
                            if d2d_jobs:
                                o_, i_ = d2d_jobs.pop(0)
                                kb.dma(sp, o_, i_, reads=[], writes=[outB], sem_buf=d2dB, partial=True)
                kb.barrier()
            with ExitStack() as ph2:
                sink = FSink(ph2)
                specs = [wspec(w2d, 256 * c, 256, nsl=3) for c in range(8)]
                st = WStream(ring, specs, max_la=1)
                rot = Rot([0, 1, 2, 3])
                for c in range(8):
                    v2, b2, _ = st.get(c)
                    for jj in range(2):
                        m = 2 * c + jj
                        f_, fB_ = sink.buf(m)
                        for (c0, n) in COLT:
                            bank, bankB = rot.next()
                            for kt in range(NFF):
                                mm(bank[:, 0:n], v2[:, kt, jj * 128:(jj + 1) * 128], g[:, kt, c0:c0 + n], kt == 0, kt == NFF - 1,
                                   b2 + [gB[kt]], [bankB], signal=(kt == NFF - 1))
                            kb.op(act, lambda f_=f_, bank=bank, c0=c0, n=n: S.copy(out=f_[:, c0:c0 + n], in_=bank[:, 0:n]), [bankB], [fB_])
                        sink.finish_tile(m)
                    st.done(c)
                sink.flush()
                kb.barrier()

    MIX_STOP = globals().get("MIX_STOP", 9)

    def mixer():
        INV_SQRT_E = 128 ** -0.5
        gath3 = gath.rearrange("(r p) n -> r p n", p=128)
        with ExitStack() as ph:
            def psb(name, shape, dt, st=None):
                return (st or ph).enter_context(SBT("m_" + name, list(shape), dt))
            m1 = psb("m1", [128, NFT, T], BF16)
            m1B = [Buf(f"m1_{i}") for i in range(NFT)]
            ONLY_MD = globals().get("ONLY_MD", False)
            if ONLY_MD:
                ccB = Buf("cc")
                kb.custom(pool, lambda: G.collective_compute("AllGather", ALU.bypass, replica_groups=[list(range(NCORES))],
                                                             ins=[send], outs=[gath]), [sendB], [gathB], ccB, 1)
                kb.barrier()
            if not ONLY_MD:
                phh = ExitStack()
                h = psb("mxh", [128, NFT, T], BF16, phh)
                hB = [Buf(f"mh{i}") for i in range(NFT)]
                build_mods(1, 1.0)
                with ExitStack() as p0:
                    prologue(p0, h, hB)
                    hsem = Buf("hsem")
                    for i in range(NFT):
                        kb.dma(sp, HD[i], h[:, i, :], reads=[hB[i]], writes=[HDb[i]], sem_buf=hsem)
                    kb.group_total(hsem, HDb + hB)
                    kb.barrier()

                with ExitStack() as pa:
                    UY = psb("UY", [128, 9, TPH], F32, pa)
                    UYB = [Buf(f"UY{i}") for i in range(9)]
                    Us = psb("Us", [128, 8, 4, 34], F32, pa); UsB = Buf("Us")
                    stash = psb("stash", [128, 8, 46], F32, pa); stashB = Buf("stash")
                    specs = []
                    for cp in range(4):
                        specs.append(wspec(w_in, W_IN_OFF["a"] + 256 * cp, 256))
                        specs.append(wspec(w_in, W_IN_OFF["b"] + 256 * cp, 256))
                    for c in range(18):
                        specs.append(wspec(w_in, W_IN_OFF["q"] + 256 * c, 256))
                    st = WStream(ring, specs, max_la=3)
                    with ExitStack() as pa0:
                        sg = [psb(f"msg{i}", [128, 512], F32, pa0) for i in range(2)]
                        sgB = [Buf(f"msg{i}") for i in range(2)]
                        stt = psb("stt", [120, CONV_CH], F32, pa0); sttB = Buf("stt")
                        PO = psb("PO", [46, CONV_CH], F32, pa0); POB = Buf("PO")

                        kb.dma(sp, stt[:], state_d.rearrange("b r c -> (b r) c"), writes=[sttB], sem_buf=sttB)
                        rot = Rot([0, 1, 2, 3, 4, 5])
                        for ct in range(8):
                            bank, bankB = rot.next()
                            tr(bank[:, 0:120], stt[0:120, ct * 128:(ct + 1) * 128], ident[0:120, 0:120], [sttB, identB], [bankB])
                            kb.op(act, lambda ct=ct, bank=bank: S.copy(out=Us[:, ct, :, 0:30], in_=bank[:, 0:120].rearrange("p (b r) -> p b r", b=4)),
                                  [bankB], [UsB])
                        k = 0
                        for cp in range(4):
                            va, ba, _ = st.get(2 * cp)
                            vb, bb, _ = st.get(2 * cp + 1)
                            for jj in range(2):
                                ct = 2 * cp + jj
                                U = UY[:, ct + 1, :]
                                UB = UYB[ct + 1]
                                for ci, (c0, n) in enumerate(COLT):
                                    bA, bAB = rot.next()
                                    bBk, bBB = rot.next()
                                    for kt in range(16):
                                        mm(bA[:, 0:n], va[:, kt, jj * 128:(jj + 1) * 128], h[:, kt, c0:c0 + n], kt == 0, kt == 15,
                                           ba + [hB[kt]], [bAB], signal=(kt == 15))
                                    for kt in range(16):
                                        mm(bBk[:, 0:n], vb[:, kt, jj * 128:(jj + 1) * 128], h[:, kt, c0:c0 + n], kt == 0, kt == 15,
                                           bb + [hB[kt]], [bBB], signal=(kt == 15))
                                    s_, sB_ = sg[k % 2], sgB[k % 2]
                                    k += 1
                                    kb.op(act, lambda s_=s_, bBk=bBk, n=n: S.activation(out=s_[:, 0:n], in_=bBk[:, 0:n], func=AF.Sigmoid),
                                          [bBB], [sB_])
                                    if ci < 2:
                                        kb.op(dve, lambda s_=s_, bA=bA, U=U, c0=c0, n=n: V.tensor_tensor(out=U[:, 30 + c0:30 + c0 + n], in0=s_[:, 0:n],
                                                                                                     in1=bA[:, 0:n], op=ALU.mult),
                                              [sB_, bAB, UB], [UB])
                                    else:
                                        kb.op(dve, lambda s_=s_, bA=bA, U=U: V.scalar_tensor_tensor(out=U[:, 0:30], in0=bA[:, 0:30], scalar=flags[:, 2:3],
                                                                                                   in1=s_[:, 0:30], op0=ALU.mult, op1=ALU.mult),
                                              [sB_, bAB, flagsB, UB], [UB])
                                        kb.op(dve, lambda s_=s_, bA=bA, ct=ct: V.tensor_tensor(out=Us[:, ct, :, 30:34],
                                                                                              in0=s_[:, 30:46].rearrange("p (b i) -> p b i", b=4),
                                                                                              in1=bA[:, 30:46].rearrange("p (b i) -> p b i", b=4), op=ALU.mult),
                                              [sB_, bAB, UsB], [UsB])
                                kb.op(act, lambda ct=ct, U=U: S.copy(out=stash[:, ct, 0:30], in_=U[:, TPH - 30:TPH]), [UB, stashB], [stashB])
                                kb.op(act, lambda ct=ct: S.copy(out=stash[:, ct, 30:46].rearrange("p (b i) -> p b i", b=4), in_=Us[:, ct, :, 30:34]),
                                      [UsB, stashB], [stashB])
                            st.done(2 * cp)
                            st.done(2 * cp + 1)
                        for half in range(2):
                            bank, bankB = rot.next()
                            for jj in range(4):
                                ct = half * 4 + jj
                                tr(bank[0:46, jj * 128:(jj + 1) * 128], stash[:, ct, :], ident[:, :], [stashB, identB], [bankB], signal=(jj == 3))
                            kb.op(act, lambda half=half, bank=bank: S.copy(out=PO[0:46, half * 512:(half + 1) * 512], in_=bank[0:46, 0:512]), [bankB, POB], [POB])
                        kb.dma(sp, pconv_o, PO[0:30, :], reads=[POB], writes=[outB], sem_buf=POB, partial=True)
                        for b in range(4):
                            kb.dma(sp, sconv_o[b, 26:30, :], PO[30 + 4 * b:34 + 4 * b, :], reads=[POB], writes=[outB], sem_buf=POB, partial=True)

                        kb.barrier()
                    with ExitStack() as pa1:
                        qb16 = [psb(f"qb16_{i}", [128, T], BF16, pa1) for i in range(2)]
                        qb16B = [Buf(f"qb16_{i}") for i in range(2)]
                        f32s = [psb(f"f32s_{i}", [128, T], F32, pa1) for i in range(2)]
                        f32sB = [Buf(f"f32s_{i}") for i in range(2)]
                        KO = psb("KO", [128, 8, 256], F32, pa1); KOB = Buf("KO")
                        KS = psb("KS", [16, 256], F32, pa1); KSB = Buf("KS")
                        dwk = vecs[:, V_DWK:V_DWK + 248].rearrange("p (c j) -> p c j", c=8)
                        for ct in range(8):
                            U = UY[:, ct + 1, :]; UB = UYB[ct + 1]
                            Y = UY[:, ct, :]; YB = UYB[ct]
                            Ys = Y[:, TP:TP + 16].rearrange("p (b i) -> p b i", b=4)
                            kb.op(dve, lambda U=U, Y=Y, ct=ct: V.tensor_scalar(out=Y[:, 0:TP], in0=U[:, 0:TP], scalar1=dwk[:, ct, 0:1],
                                                                              scalar2=vecs[:, V_DWB + ct:V_DWB + ct + 1], op0=ALU.mult, op1=ALU.add),
                                  [UB, vecsB, YB], [YB])
                            kb.op(dve, lambda Ys=Ys, ct=ct: V.tensor_scalar(out=Ys, in0=Us[:, ct, :, 0:4], scalar1=dwk[:, ct, 0:1],
                                                                           scalar2=vecs[:, V_DWB + ct:V_DWB + ct + 1], op0=ALU.mult, op1=ALU.add),
                                  [UsB, vecsB, YB], [YB])
                            for j in range(1, 31):
                                kb.op(dve, lambda U=U, Y=Y, ct=ct, j=j: V.scalar_tensor_tensor(out=Y[:, 0:TP], in0=U[:, j:j + TP], scalar=dwk[:, ct, j:j + 1],
                                                                                              in1=Y[:, 0:TP], op0=ALU.mult, op1=ALU.add),
                                      [UB, vecsB, YB], [YB])
                                kb.op(dve, lambda Ys=Ys, ct=ct, j=j: V.scalar_tensor_tensor(out=Ys, in0=Us[:, ct, :, j:j + 4], scalar=dwk[:, ct, j:j + 1],
                                                                                           in1=Ys, op0=ALU.mult, op1=ALU.add),
                                      [UsB, vecsB, YB], [YB])

                        rotq = Rot([0, 1, 2, 3])
                        rott = Rot([4, 5])
                        for c in range(18):
                            vw, bw, _ = st.get(8 + c)
                            for jj in range(2):
                                mt = 2 * c + jj
                                which, gh = mt // 12, mt % 12
                                g_, hh = gh // 4, gh % 4
                                qb_, qbB_ = qb16[mt % 2], qb16B[mt % 2]
                                fs_, fsB_ = f32s[mt % 2], f32sB[mt % 2]
                                for (c0, n) in COLT:
                                    bank, bankB = rotq.next()
                                    for kt in range(16):
                                        mm(bank[:, 0:n], vw[:, kt, jj * 128:(jj + 1) * 128], h[:, kt, c0:c0 + n], kt == 0, kt == 15,
                                           bw + [hB[kt]], [bankB], signal=(kt == 15))
                                    if which == 0:
                                        kb.op(act, lambda qb_=qb_, bank=bank, c0=c0, n=n: S.mul(out=qb_[:, c0:c0 + n], in_=bank[:, 0:n], mul=INV_SQRT_E),
                                              [bankB, qbB_], [qbB_])
                                    else:
                                        kb.op(act, lambda fs_=fs_, bank=bank, c0=c0, n=n: S.copy(out=fs_[:, c0:c0 + n], in_=bank[:, 0:n]), [bankB, fsB_], [fsB_])
                                if which > 0:
                                    kb.op(act, lambda qb_=qb_, fs_=fs_: S.copy(out=qb_[:], in_=fs_[:]), [fsB_, qbB_], [qbB_])
                                kb.dma(sp, QKV[mt], qb_[:], reads=[qbB_], writes=[QKVb[mt]], sem_buf=qbB_)
                                if which > 0:
                                    H_ = HG[g_]
                                    off = (which - 1) * SEND_HALF + SEND_BASE[g_] + hh * H_
                                    kb.dma(sp, send[:, off:off + H_], qb_[:, TP - H_:TP], reads=[qbB_], writes=[sendB], sem_buf=qbB_, partial=True)
                                    hp, hq = hh // 2, hh % 2
                                    for q0 in (0, 4):
                                        bank, bankB = rott.next()
                                        for j4 in range(4):
                                            tt = q0 + j4
                                            tr(bank[:, j4 * 128:(j4 + 1) * 128], fs_[:, tt * 128:(tt + 1) * 128], ident[:, :], [fsB_, identB], [bankB],
                                               signal=(j4 == 3))
                                        kb.op(act, lambda bank=bank, q0=q0, hq=hq: S.copy(out=KO[:, q0:q0 + 4, hq * 128:(hq + 1) * 128],
                                                                                         in_=bank[:, 0:512].rearrange("p (a b) -> p a b", a=4)),
                                              [bankB, KOB], [KOB])
                                    bank, bankB = rott.next()
                                    tr(bank[0:16, 0:128], fs_[:, TPH:T], ident[:, :], [fsB_, identB], [bankB])
                                    kb.op(act, lambda bank=bank, hq=hq: S.copy(out=KS[0:16, hq * 128:(hq + 1) * 128], in_=bank[0:16, 0:128]), [bankB, KSB], [KSB])
                                    if hq == 1:
                                        dst = kout if which == 1 else vout
                                        kb.dma(sp, dst[g_][:, hp * 256:(hp + 1) * 256].rearrange("(t p) c -> p t c", p=128), KO[:, :, :],
                                               reads=[KOB], writes=[outB], sem_buf=KOB, partial=True)
                                        so = sk_o if which == 1 else sv_o
                                        Lg = GROUPS[g_][0]
                                        for b in range(4):
                                            kb.dma(sp, so[g_][b, Lg - 4:Lg, hp * 256:(hp + 1) * 256], KS[4 * b:4 * b + 4, :], reads=[KSB], writes=[outB],
                                                   sem_buf=KSB, partial=True)
                            st.done(8 + c)
                        ccB = Buf("cc")
                        kb.custom(pool, lambda: G.collective_compute("AllGather", ALU.bypass, replica_groups=[list(range(NCORES))],
                                                                     ins=[send], outs=[gath]), [sendB], [gathB], ccB, 1)
                        kb.barrier()
                    if MIX_STOP <= 1:
                        return False

                    Sx = psb("Sx", [128, 8, T], BF16, pa)
                    SxB = [Buf(f"Sx{i}") for i in range(8)]
                    with ExitStack() as pb:
                        mu = psb("lnmu", [128, TP + 16], F32, pb); muB = Buf("mu")
                        rs = psb("lnrs", [128, TP + 16], F32, pb); rsB = Buf("rs")
                        yb = [psb(f"lnyb{i}", [128, TP + 16], BF16, pb) for i in range(2)]
                        ybB = [Buf(f"lnyb{i}") for i in range(2)]
                        ysq = [psb(f"lnysq{i}", [128, TP + 16], BF16, pb) for i in range(2)]
                        ysqB = [Buf(f"lnysq{i}") for i in range(2)]
                        LC = [(0, 512), (512, 512), (1024, 16)]
                        for ct in range(8):
                            Y = UY[:, ct, :]; YB = UYB[ct]
                            a_, aB_, q_, qB_ = yb[ct % 2], ybB[ct % 2], ysq[ct % 2], ysqB[ct % 2]
                            kb.op(act, lambda a_=a_, Y=Y: S.copy(out=a_[:], in_=Y[:, 0:TP + 16]), [YB], [aB_])
                            kb.op(act, lambda q_=q_, Y=Y: S.activation(out=q_[:], in_=Y[:, 0:TP + 16], func=AF.Square), [YB], [qB_])
                            for ci, (c0, n) in enumerate(LC):
                                mm(PS[ci][:, 0:n], ones_bf[:], a_[:, c0:c0 + n], ct == 0, ct == 7, [onesB, aB_], [PSb[ci]], signal=False)
                                mm(PS[3 + ci][:, 0:n], ones_bf[:], q_[:, c0:c0 + n], ct == 0, ct == 7, [onesB, qB_], [PSb[3 + ci]], signal=(ci == 2))
                        for ci, (c0, n) in enumerate(LC):
                            kb.op(act, lambda ci=ci, c0=c0, n=n: S.mul(out=mu[:, c0:c0 + n], in_=PS[ci][:, 0:n], mul=1.0 / CONV_CH), [PSb[ci]], [muB])
                        kb.op(dve, lambda: V.tensor_tensor(out=rs[:], in0=mu[:], in1=mu[:], op=ALU.mult), [muB], [rsB])
                        for ci, (c0, n) in enumerate(LC):
                            kb.op(dve, lambda ci=ci, c0=c0, n=n: V.scalar_tensor_tensor(out=rs[:, c0:c0 + n], in0=PS[3 + ci][:, 0:n], scalar=1.0 / CONV_CH,
                                                                                       in1=rs[:, c0:c0 + n], op0=ALU.mult, op1=ALU.subtract),
                                  [PSb[3 + ci], rsB], [rsB])
                        kb.op(act, lambda: S.activation(out=rs[:], in_=rs[:], func=AF.Sqrt, bias=epsT[:, 0:1], scale=1.0), [rsB, epsB], [rsB])
                        kb.op(dve, lambda: V.reciprocal(out=rs[:], in_=rs[:]), [rsB], [rsB])
                        for ct in range(8):
                            Y = UY[:, ct, :]; YB = UYB[ct]
                            kb.op(dve, lambda Y=Y: V.tensor_tensor(out=Y[:, 0:TP + 16], in0=Y[:, 0:TP + 16], in1=mu[:], op=ALU.subtract), [YB, muB], [YB])
                            kb.op(dve, lambda Y=Y: V.tensor_tensor(out=Y[:, 0:TP + 16], in0=Y[:, 0:TP + 16], in1=rs[:], op=ALU.mult), [YB, rsB], [YB])
                            kb.op(dve, lambda ct=ct: V.memset(Sx[:, ct, TP:TPH], 0.0), [SxB[ct]], [SxB[ct]])
                            kb.op(act, lambda ct=ct, Y=Y: S.activation(out=Sx[:, ct, 0:TP], in_=Y[:, 0:TP], func=AF.Silu,
                                                                       bias=vecs[:, V_LNB + ct:V_LNB + ct + 1], scale=vecs[:, V_LNG + ct:V_LNG + ct + 1]),
                                  [YB, vecsB, SxB[ct]], [SxB[ct]])
                            kb.op(act, lambda ct=ct, Y=Y: S.activation(out=Sx[:, ct, TPH:T], in_=Y[:, TP:TP + 16], func=AF.Silu,
                                                                       bias=vecs[:, V_LNB + ct:V_LNB + ct + 1], scale=vecs[:, V_LNG + ct:V_LNG + ct + 1]),
                                  [YB, vecsB, SxB[ct]], [SxB[ct]])
                        kb.barrier()
                    with ExitStack() as pc:
                        sg = [psb(f"csg{i}", [128, 512], F32, pc) for i in range(2)]
                        sgB = [Buf(f"csg{i}") for i in range(2)]
                        specs = []
                        for q4 in range(4):
                            specs.append(wspec(w_co, 512 * q4, 512))
                            specs.append(wspec(w_in, W_IN_OFF["gc"] + 512 * q4, 256))
                            specs.append(wspec(w_in, W_IN_OFF["gc"] + 512 * q4 + 256, 256))
                        st = WStream(ring, specs, max_la=3)
                        rot = Rot([0, 1, 2, 3, 4, 5])
                        k = 0
                        for q4 in range(4):
                            vco, bco, _ = st.get(3 * q4)
                            for half in range(2):
                                vg, bg_, _ = st.get(3 * q4 + 1 + half)
                                for jj in range(2):
                                    mt = 4 * q4 + 2 * half + jj
                                    for (c0, n) in COLT:
                                        bC, bCB = rot.next()
                                        bG, bGB = rot.next()
                                        for kt in range(8):
                                            mm(bC[:, 0:n], vco[:, kt, (2 * half + jj) * 128:(2 * half + jj + 1) * 128], Sx[:, kt, c0:c0 + n], kt == 0, kt == 7,
                                               bco + [SxB[kt]], [bCB], signal=(kt == 7))
                                        for kt in range(16):
                                            mm(bG[:, 0:n], vg[:, kt, jj * 128:(jj + 1) * 128], h[:, kt, c0:c0 + n], kt == 0, kt == 15,
                                               bg_ + [hB[kt]], [bGB], signal=(kt == 15))
                                        s_, sB_ = sg[k % 2], sgB[k % 2]
                                        k += 1
                                        kb.op(act, lambda s_=s_, bG=bG, n=n: S.activation(out=s_[:, 0:n], in_=bG[:, 0:n], func=AF.Sigmoid), [bGB], [sB_])
                                        kb.op(dve, lambda s_=s_, bC=bC, mt=mt, c0=c0, n=n: V.tensor_tensor(out=m1[:, mt, c0:c0 + n], in0=s_[:, 0:n], in1=bC[:, 0:n],
                                                                                                         op=ALU.mult), [sB_, bCB, m1B[mt]], [m1B[mt]])
                                st.done(3 * q4 + 1 + half)
                            st.done(3 * q4)
                        kb.barrier()

                phh.close()
            if MIX_STOP <= 2:
                return False
            att = psb("att_o", [128, 4, T], BF16)
            attB = [Buf(f"att{i}") for i in range(4)]
            kb.op(dve, lambda: V.memset(att[:, :, :], 0.0), attB, attB)
            for k_, (hd, hdB) in ((1, (halo1, halo1B)), (2, (halo2, halo2B))):
                pv_ = (G.partition_id() + (8 - k_)) % 8
                kb.dma(pool, hd, gath3[bass.ds(pv_, 1), :, :], reads=[gathB], writes=[hdB], sem_buf=hdB)
            with ExitStack() as pd:
                accN = psb("accN", [128, TP], F32, pd); accNB = Buf("accN")
                accD = psb("accD", [128, TP], F32, pd); accDB = Buf("accD")
                Eh = [psb(f"Eh{i}", [128, 3, 256], F32, pd) for i in range(2)]
                EhB = [Buf(f"Eh{i}") for i in range(2)]
                Em = [psb(f"Em{i}", [128, 3, 256], F32, pd) for i in range(2)]
                EmB = [Buf(f"Em{i}") for i in range(2)]
                Em2 = [psb(f"Em2_{i}", [128, 64], F32, pd) for i in range(2)]
                Em2B = [Buf(f"Em2_{i}") for i in range(2)]
                sets = []
                for s_i in range(2):
                    d_ = {}
                    for nm, w_ in (("q", T), ("kl", T), ("vl", T), ("kh", 2048), ("vh", 2048)):
                        d_[nm] = psb(f"a{nm}{s_i}", [128, w_], BF16, pd)
                        d_[nm + "B"] = Buf(f"a{nm}{s_i}")
                    sets.append(d_)
                VT = psb("VT", [128, 48, 128], BF16, pd); VTB = Buf("VT")
                ex = [psb(f"ex{i}", [128, 128], F32, pd) for i in range(3)]
                exB = [Buf(f"ex{i}") for i in range(3)]
                pT = [psb(f"pT{i}", [128, 128], BF16, pd) for i in range(3)]
                pTB = [Buf(f"pT{i}") for i in range(3)]
                set_i = 0
                for hh in range(4):
                    E_, EB_, Em_, EmB_, E2_, E2B_ = Eh[hh % 2], EhB[hh % 2], Em[hh % 2], EmB[hh % 2], Em2[hh % 2], Em2B[hh % 2]
                    srcE = bass.AP(big_d, hh * 130 * 384, [[383, 128], [4 * 130 * 384, 3], [1, 256]])
                    kb.dma(sp, E_[:, :, :], srcE, reads=[bigB], writes=[EB_], sem_buf=EB_)
                    kb.op(dve, lambda E_=E_, Em_=Em_: V.tensor_scalar(out=Em_[:, :, :], in0=E_[:, :, :], scalar1=flags[:, 0:1], scalar2=0.0,
                                                                     op0=ALU.mult, op1=ALU.add), [EB_, flagsB, EmB_], [EmB_])
                    kb.op(dve, lambda E_=E_, E2_=E2_: V.tensor_scalar(out=E2_[:, :], in0=E_[:, 2, 128:192], scalar1=flags[:, 1:2], scalar2=0.0,
                                                                     op0=ALU.mult, op1=ALU.add), [EB_, flagsB, E2B_], [E2B_])
                    for g_ in range(3):
                        d_ = sets[set_i % 2]
                        set_i += 1
                        gh = 4 * g_ + hh
                        H_ = HG[g_]
                        dd = GROUPS[g_][1]
                        kb.dma(sp, d_["q"][:], QKV[gh], reads=[QKVb[gh]], writes=[d_["qB"]], sem_buf=d_["qB"])
                        kb.dma(sp, d_["kl"][:], QKV[12 + gh], reads=[QKVb[12 + gh]], writes=[d_["klB"]], sem_buf=d_["klB"])
                        kb.dma(sp, d_["vl"][:], QKV[24 + gh], reads=[QKVb[24 + gh]], writes=[d_["vlB"]], sem_buf=d_["vlB"])
                        for nm, half in (("kh", 0), ("vh", 1)):
                            off = half * SEND_HALF + SEND_BASE[g_] + hh * H_
                            if g_ < 2:
                                kb.dma(sp, d_[nm][:, 0:H_], halo1[:, off:off + H_], reads=[halo1B], writes=[d_[nm + "B"]],
                                       sem_buf=d_[nm + "B"])
                            else:
                                kb.dma(sp, d_[nm][:, 0:1024], halo2[:, off:off + 1024], reads=[halo2B], writes=[d_[nm + "B"]],
                                       sem_buf=d_[nm + "B"])
                                kb.dma(sp, d_[nm][:, 1024:2048], halo1[:, off:off + 1024], reads=[halo1B], writes=[d_[nm + "B"]],
                                       sem_buf=d_[nm + "B"], partial=True)
                        tiles = {}
                        blocks = []
                        if g_ == 0:
                            tiles["H"] = ("h", 0, 1, 128)
                            for j in range(8):
                                tiles[("L", j)] = ("l", 128 * j, 1, 128)
                            for j in range(8):
                                prev = ("H" if j == 0 else ("L", j - 1))
                                blocks.append((128 * j, 1, 128, [(prev, 128, "m" if j == 0 else "e"), (("L", j), 0, "e")]))
                        elif g_ == 1:
                            for r in range(4):
                                tiles[("H", r)] = ("h", r, 4, 128)
                                tiles[("L", r, 0)] = ("l", r, 4, 128)
                                tiles[("L", r, 1)] = ("l", r + 512, 4, 128)
                            for r in range(4):
                                for qb in range(2):
                                    prev = (("H", r) if qb == 0 else ("L", r, 0))
                                    blocks.append((r + 512 * qb, 4, 128, [(prev, 128, "m" if qb == 0 else "e"), (("L", r, qb), 0, "e")]))
                        else:
                            for r in range(16):
                                tiles[("H2", r)] = ("h", r, 16, 64)
                                tiles[("H1", r)] = ("h", 1024 + r, 16, 64)
                                tiles[("L", r)] = ("l", r, 16, 64)
                            for r in range(16):
                                blocks.append((r, 16, 64, [(("H2", r), 128, "m2"), (("H1", r), 64, "m"), (("L", r), 0, "e")]))
                        tkeys = list(tiles.keys())
                        slot = {tk: i for i, tk in enumerate(tkeys)}
                        for i0 in range(0, len(tkeys), 4):
                            grp = tkeys[i0:i0 + 4]
                            hb = (i0 // 4) % 2
                            for j4, tk in enumerate(grp):
                                srcn, c0, stp, nk = tiles[tk]
                                vsrc = d_["vh"] if srcn == "h" else d_["vl"]
                                vB_ = d_["vhB"] if srcn == "h" else d_["vlB"]
                                tr(PB[0:nk, hb * 512 + j4 * 128: hb * 512 + (j4 + 1) * 128], vsrc[:, c0:c0 + stp * (nk - 1) + 1:stp], identb[:, :],
                                   [vB_, identbB], [PBb[hb]], signal=(j4 == len(grp) - 1))
                            nk = tiles[grp[0]][3]
                            kb.op(act, lambda i0=i0, hb=hb, nk=nk, ng=len(grp): S.copy(out=VT[0:nk, i0:i0 + ng, :],
                                                                                       in_=PB[0:nk, hb * 512: hb * 512 + ng * 128].rearrange("p (a b) -> p a b", a=ng)),
                                  [PBb[hb], VTB], [VTB])
                        work = []
                        for bi, (qs, qstep, nq, tl) in enumerate(blocks):
                            for ti, (tk, off, esel) in enumerate(tl):
                                work.append((bi, ti, len(tl), qs, qstep, nq, tk, off, esel))
                        rS = Rot([0, 1, 2])
                        nd = [(PS[3], PSb[3], PS[4], PSb[4]), (PS[5], PSb[5], PS[6], PSb[6])]
                        sc_state = {}

                        def emit_score(wi):
                            bi, ti, nt, qs, qstep, nq, tk, off, esel = work[wi]
                            srcn, c0, stp, nk = tiles[tk]
                            ksrc = d_["kh"] if srcn == "h" else d_["kl"]
                            kB_ = d_["khB"] if srcn == "h" else d_["klB"]
                            bank, bankB = rS.next()
                            mm(bank[0:nk, 0:nq], ksrc[:, c0:c0 + stp * (nk - 1) + 1:stp], d_["q"][:, qs:qs + qstep * (nq - 1) + 1:qstep], True, True,
                               [kB_, d_["qB"]], [bankB], signal=True)
                            e_, eB_, p_, pB_ = ex[wi % 3], exB[wi % 3], pT[wi % 3], pTB[wi % 3]
                            kb.op(act, lambda: S.activation(out=e_[0:nk, 0:nq], in_=bank[0:nk, 0:nq], func=AF.Exp), [bankB], [eB_])
                            if esel == "e":
                                Es, EsB = E_[0:nk, g_, off:off + nq], EB_
                            elif esel == "m":
                                Es, EsB = Em_[0:nk, g_, off:off + nq], EmB_
                            else:
                                Es, EsB = E2_[0:nk, 0:nq], E2B_
                            kb.op(dve, lambda: V.tensor_tensor(out=p_[0:nk, 0:nq], in0=e_[0:nk, 0:nq], in1=Es, op=ALU.mult), [eB_, EsB], [pB_])
                            sc_state[wi] = (p_, pB_, nk)

                        def emit_pv(wi):
                            bi, ti, nt, qs, qstep, nq, tk, off, esel = work[wi]
                            p_, pB_, nk = sc_state.pop(wi)
                            nP, nPB, dP, dPB = nd[bi % 2]
                            mm(nP[:, 0:nq], VT[0:nk, slot[tk], :], p_[0:nk, 0:nq], ti == 0, ti == nt - 1, [VTB, pB_], [nPB], signal=False)
                            mm(dP[:, 0:nq], ones_bf[0:nk, :], p_[0:nk, 0:nq], ti == 0, ti == nt - 1, [onesB, pB_], [dPB], signal=True)
                            if ti == nt - 1:
                                oN = accN[:, qs:qs + qstep * (nq - 1) + 1:qstep]
                                oD = accD[:, qs:qs + qstep * (nq - 1) + 1:qstep]
                                if g_ == 0:
                                    kb.op(act, lambda: S.copy(out=oN, in_=nP[:, 0:nq]), [nPB, accNB], [accNB])
                                    kb.op(dve, lambda: V.tensor_copy(out=oD, in_=dP[:, 0:nq]), [dPB, accDB], [accDB])
                                else:
                                    kb.op(dve, lambda: V.tensor_tensor(out=oN, in0=nP[:, 0:nq], in1=oN, op=ALU.add), [nPB, accNB], [accNB])
                                    kb.op(dve, lambda: V.tensor_tensor(out=oD, in0=dP[:, 0:nq], in1=oD, op=ALU.add), [dPB, accDB], [accDB])

                        emit_score(0)
                        for wi in range(len(work)):
                            if wi + 1 < len(work):
                                emit_score(wi + 1)
                            emit_pv(wi)
                    kb.op(dve, lambda: V.reciprocal(out=accD[:, :], in_=accD[:, :]), [accDB], [accDB])
                    kb.op(dve, lambda hh=hh: V.tensor_tensor(out=att[:, hh, 0:TP], in0=accN[:, :], in1=accD[:, :], op=ALU.mult),
                          [accNB, accDB, attB[hh]], [attB[hh]])
                kb.barrier()

            if MIX_STOP <= 3:
                return False
            with ExitStack() as ps_:
                qkvn = psb("qkvn", [128, 36, NS], BF16, ps_); qkvnB = Buf("qkvn")
                for w3_ in range(3):
                    kb.dma(sp, qkvn[:, 12 * w3_:12 * w3_ + 12, :], QKV[12 * w3_:12 * w3_ + 12, :, TPH:T].rearrange("m p c -> p m c"),
                           reads=QKVb[12 * w3_:12 * w3_ + 12], writes=[qkvnB], sem_buf=qkvnB, partial=(w3_ > 0))
                Esm = psb("Esm", [128, 12, 4], F32, ps_); EsmB = Buf("Esm")
                En = psb("En", [16, 12, 16], F32, ps_); EnB = Buf("En")
                msk = psb("msk", [16, 32], F32, ps_); mskB = Buf("msk")
                kb.dma(sp, Esm[:, :, :], bass.AP(big_d, 128, [[383, 128], [130 * 384, 12], [1, 4]]), reads=[bigB], writes=[EsmB], sem_buf=EsmB)
                kb.dma(sp, En[:, :, :], bass.AP(big_d, 0, [[383, 16], [130 * 384, 12], [1, 16]]), reads=[bigB], writes=[EnB], sem_buf=EnB)
                kb.dma(sp, msk[:, :], masks_d, writes=[mskB], sem_buf=mskB)
                kb.op(dve, lambda: V.tensor_tensor(out=En[:, 0:4, :], in0=En[:, 0:4, :], in1=msk[:, 0:16].unsqueeze(1).to_broadcast([16, 4, 16]), op=ALU.mult),
                      [EnB, mskB], [EnB])
                kb.op(dve, lambda: V.tensor_tensor(out=En[:, 4:12, :], in0=En[:, 4:12, :], in1=msk[:, 16:32].unsqueeze(1).to_broadcast([16, 8, 16]), op=ALU.mult),
                      [EnB, mskB], [EnB])
                aN = psb("aNs", [128, 4, NS], F32, ps_); aNB = Buf("aNs")
                aD = psb("aDs", [128, 4, NS], F32, ps_); aDB = Buf("aDs")
                vnT = psb("vnT", [16, 12, 128], BF16, ps_); vnTB = Buf("vnT")
                exs = [psb(f"exs{i}", [128, 16], F32, ps_) for i in range(2)]
                exsB = [Buf(f"exs{i}") for i in range(2)]
                pTs = [psb(f"pTs{i}", [128, 16], BF16, ps_) for i in range(2)]
                pTsB = [Buf(f"pTs{i}") for i in range(2)]
                Kc = [psb(f"Kc{i}", [128, 4, 512], BF16, ps_) for i in range(2)]
                KcB = [Buf(f"Kc{i}") for i in range(2)]
                Vc = [psb(f"Vc{i}", [128, 4, 512], BF16, ps_) for i in range(2)]
                VcB = [Buf(f"Vc{i}") for i in range(2)]
                KcT = [psb(f"KcT{i}", [128, 4, 4, 128], BF16, ps_) for i in range(2)]
                KcTB = [Buf(f"KcT{i}") for i in range(2)]
                for i0 in range(0, 12, 4):
                    hb = (i0 // 4) % 2
                    for j4 in range(4):
                        tr(PB[0:16, hb * 512 + j4 * 128: hb * 512 + (j4 + 1) * 128], qkvn[:, 24 + i0 + j4, :], identb[:, :], [qkvnB, identbB], [PBb[hb]],
                           signal=(j4 == 3))
                    kb.op(act, lambda i0=i0, hb=hb: S.copy(out=vnT[0:16, i0:i0 + 4, :], in_=PB[0:16, hb * 512:hb * 512 + 512].rearrange("p (a b) -> p a b", a=4)),
                          [PBb[hb], vnTB], [vnTB])
                rS = Rot([0, 1, 2])
                rN = Rot([3, 4])
                rD = Rot([5, 6])
                first = [True] * 4
                cnt = 0

                def accum(hh, c0, n, nP, nPB, dP, dPB):
                    oN, oD = aN[:, hh, c0:c0 + n], aD[:, hh, c0:c0 + n]
                    if first[hh]:
                        kb.op(dve, lambda: V.tensor_copy(out=oN, in_=nP[:, 0:n]), [nPB, aNB], [aNB])
                        kb.op(dve, lambda: V.tensor_copy(out=oD, in_=dP[:, 0:n]), [dPB, aDB], [aDB])
                    else:
                        kb.op(dve, lambda: V.tensor_tensor(out=oN, in0=nP[:, 0:n], in1=oN, op=ALU.add), [nPB, aNB], [aNB])
                        kb.op(dve, lambda: V.tensor_tensor(out=oD, in0=dP[:, 0:n], in1=oD, op=ALU.add), [dPB, aDB], [aDB])

                for hh in range(4):
                    for g_ in range(3):
                        gh = 4 * g_ + hh
                        bank, bankB = rS.next()
                        mm(bank[0:16, 0:16], qkvn[:, 12 + gh, :], qkvn[:, gh, :], True, True, [qkvnB], [bankB], signal=True)
                        e_, eB_, p_, pB_ = exs[cnt % 2], exsB[cnt % 2], pTs[cnt % 2], pTsB[cnt % 2]
                        cnt += 1
                        kb.op(act, lambda e_=e_, bank=bank: S.activation(out=e_[0:16, 0:16], in_=bank[0:16, 0:16], func=AF.Exp), [bankB], [eB_])
                        kb.op(dve, lambda e_=e_, p_=p_, gh=gh: V.tensor_tensor(out=p_[0:16, 0:16], in0=e_[0:16, 0:16], in1=En[0:16, gh, :], op=ALU.mult),
                              [eB_, EnB], [pB_])
                        nP, nPB = rN.next()
                        dP, dPB = rD.next()
                        mm(nP[:, 0:16], vnT[0:16, gh, :], p_[0:16, 0:16], True, True, [vnTB, pB_], [nPB], signal=False)
                        mm(dP[:, 0:16], ones_bf[0:16, :], p_[0:16, 0:16], True, True, [onesB, pB_], [dPB], signal=True)
                        accum(hh, 0, 16, nP, nPB, dP, dPB)
                        first[hh] = False
                li = 0
                for b in range(4):
                    for g_ in range(3):
                        Lg, dd = GROUPS[g_]
                        ni = 1 if g_ == 0 else 4
                        kc_, kcB_, vc_, vcB_, kt_, ktB_ = Kc[li % 2], KcB[li % 2], Vc[li % 2], VcB[li % 2], KcT[li % 2], KcTB[li % 2]
                        li += 1
                        for (dst, dB, srcd) in ((kc_, kcB_, ck_d[g_]), (vc_, vcB_, cv_d[g_])):
                            if g_ == 0:
                                srcap = srcd[b, :, :].rearrange("(n i) c -> n i c", i=1)
                            else:
                                srcap = srcd[b, :, :].rearrange("(n s) c -> n s c", s=dd)[:, 0:4, :]
                            kb.dma(pool, dst[:, 0:ni, :], srcap, reads=[], writes=[dB], sem_buf=dB)
                        for i in range(ni):
                            hb = i % 2
                            for hh in range(4):
                                tr(PB[:, hb * 512 + hh * 128: hb * 512 + (hh + 1) * 128], kc_[:, i, hh * 128:(hh + 1) * 128], identb[:, :], [kcB_, identbB], [PBb[hb]],
                                   signal=(hh == 3))
                            kb.op(act, lambda i=i, hb=hb, kt_=kt_: S.copy(out=kt_[:, i, :, :], in_=PB[:, hb * 512:hb * 512 + 512].rearrange("p (a b) -> p a b", a=4)),
                                  [PBb[hb], ktB_], [ktB_])
                        for hh in range(4):
                            gh = 4 * g_ + hh
                            bank, bankB = rS.next()
                            if g_ == 0:
                                mm(bank[:, 0:4], kt_[:, 0, hh, :], qkvn[:, gh, 4 * b:4 * b + 4], True, True, [ktB_, qkvnB], [bankB], signal=True)
                            else:
                                for i in range(4):
                                    mm(bank[:, i:i + 1], kt_[:, i, hh, :], qkvn[:, gh, 4 * b + i:4 * b + i + 1], True, True, [ktB_, qkvnB], [bankB], signal=(i == 3))
                            e_, eB_, p_, pB_ = exs[cnt % 2], exsB[cnt % 2], pTs[cnt % 2], pTsB[cnt % 2]
                            cnt += 1
                            kb.op(act, lambda e_=e_, bank=bank: S.activation(out=e_[:, 0:4], in_=bank[:, 0:4], func=AF.Exp), [bankB], [eB_])
                            if g_ == 0:
                                kb.op(dve, lambda e_=e_, p_=p_, gh=gh: V.tensor_tensor(out=p_[:, 0:4], in0=e_[:, 0:4], in1=Esm[:, gh, 0:4], op=ALU.mult),
                                      [eB_, EsmB], [pB_])
                            else:
                                kb.op(dve, lambda e_=e_, p_=p_, gh=gh: V.tensor_scalar(out=p_[:, 0:4], in0=e_[:, 0:4], scalar1=Esm[:, gh, 0:1], scalar2=0.0,
                                                                                     op0=ALU.mult, op1=ALU.add), [eB_, EsmB], [pB_])
                            nP, nPB = rN.next()
                            dP, dPB = rD.next()
                            if g_ == 0:
                                mm(nP[:, 0:4], vc_[:, 0, hh * 128:(hh + 1) * 128], p_[:, 0:4], True, True, [vcB_, pB_], [nPB], signal=False)
                            else:
                                for i in range(4):
                                    mm(nP[:, i:i + 1], vc_[:, i, hh * 128:(hh + 1) * 128], p_[:, i:i + 1], True, True, [vcB_, pB_], [nPB], signal=False)
                            mm(dP[:, 0:4], ones_bf[:, :], p_[:, 0:4], True, True, [onesB, pB_], [dPB], signal=True)
                            accum(hh, 4 * b, 4, nP, nPB, dP, dPB)
                kb.op(dve, lambda: V.reciprocal(out=aD[:, :, :], in_=aD[:, :, :]), [aDB], [aDB])
                kb.op(dve, lambda: V.tensor_tensor(out=att[:, :, TPH:T], in0=aN[:, :, :], in1=aD[:, :, :], op=ALU.mult), [aNB, aDB] + attB, attB)
                kb.barrier()

            if MIX_STOP <= 4:
                return False
            with ExitStack() as pe_:
                h = psb("mxh2", [128, NFT, T], BF16, pe_)
                hB = [Buf(f"mh2_{i}") for i in range(NFT)]
                hsem2 = Buf("hsem2")
                for i in range(NFT):
                    kb.dma(sp, h[:, i, :], HD[i], reads=[HDb[i]], writes=[hB[i]], sem_buf=hsem2)
                kb.group_total(hsem2, HDb + hB)
                sg = [psb(f"esg{i}", [128, 512], F32, pe_) for i in range(2)]
                sgB = [Buf(f"esg{i}") for i in range(2)]
                sink = FSink(pe_)
                specs = []
                for half in range(2):
                    specs.append(wspec(w_ao, 1024 * half, 1024))
                    for q in range(4):
                        specs.append(wspec(w_in, W_IN_OFF["ga"] + 1024 * half + 256 * q, 256))
                for c in range(8):
                    specs.append(wspec(w_out, 256 * c, 256))
                st = WStream(ring, specs, max_la=3)
                rot = Rot([0, 1, 2, 3])
                k = 0
                for half in range(2):
                    vao, bao, _ = st.get(5 * half)
                    for q in range(4):
                        vg, bg_, _ = st.get(5 * half + 1 + q)
                        for jj in range(2):
                            mt = 8 * half + 2 * q + jj
                            for (c0, n) in COLT:
                                bA, bAB = rot.next()
                                bG, bGB = rot.next()
                                for kt in range(4):
                                    mm(bA[:, 0:n], vao[:, kt, (2 * q + jj) * 128:(2 * q + jj + 1) * 128], att[:, kt, c0:c0 + n], kt == 0, kt == 3,
                                       bao + [attB[kt]], [bAB], signal=(kt == 3))
                                for kt in range(16):
                                    mm(bG[:, 0:n], vg[:, kt, jj * 128:(jj + 1) * 128], h[:, kt, c0:c0 + n], kt == 0, kt == 15,
                                       bg_ + [hB[kt]], [bGB], signal=(kt == 15))
                                s_, sB_ = sg[k % 2], sgB[k % 2]
                                k += 1
                                kb.op(act, lambda s_=s_, bG=bG, n=n: S.activation(out=s_[:, 0:n], in_=bG[:, 0:n], func=AF.Sigmoid), [bGB], [sB_])
                                kb.op(dve, lambda s_=s_, bA=bA, n=n: V.tensor_tensor(out=s_[:, 0:n], in0=s_[:, 0:n], in1=bA[:, 0:n], op=ALU.mult), [sB_, bAB], [sB_])
                                kb.op(dve, lambda s_=s_, mt=mt, c0=c0, n=n: V.tensor_tensor(out=m1[:, mt, c0:c0 + n], in0=s_[:, 0:n], in1=m1[:, mt, c0:c0 + n], op=ALU.add),
                                      [sB_, m1B[mt]], [m1B[mt]])
                        st.done(5 * half + 1 + q)
                    st.done(5 * half)
                for c in range(8):
                    vw, bw, _ = st.get(10 + c)
                    for jj in range(2):
                        m = 2 * c + jj
                        f_, fB_ = sink.buf(m)
                        for (c0, n) in COLT:
                            bank, bankB = rot.next()
                            for kt in range(16):
                                mm(bank[:, 0:n], vw[:, kt, jj * 128:(jj + 1) * 128], m1[:, kt, c0:c0 + n], kt == 0, kt == 15,
                                   bw + [m1B[kt]], [bankB], signal=(kt == 15))
                            kb.op(act, lambda f_=f_, bank=bank, c0=c0, n=n: S.copy(out=f_[:, c0:c0 + n], in_=bank[:, 0:n]), [bankB, fB_], [fB_])
                        sink.finish_tile(m)
                    st.done(10 + c)
                sink.flush()
                kb.barrier()
        epilogue(dump=dbg.get("x2"))
        return True

    ONLY_MD_G = globals().get("ONLY_MD", False)
    for c in range(0 if ONLY_MD_G else 24):
        ada_chunk(c)
    for gi in range(3):
        L = GROUPS[gi][0]
        for b in range(4):
            for r0 in range(0, L - 4, 512):
                nr = min(512, L - 4 - r0)
                d2d_jobs.append((sk_o[gi][b, r0:r0 + nr, :], ck_d[gi][b, 4 + r0:4 + r0 + nr, :]))
                d2d_jobs.append((sv_o[gi][b, r0:r0 + nr, :], cv_d[gi][b, 4 + r0:4 + r0 + nr, :]))
    for b in range(4):
        d2d_jobs.append((sconv_o[b, 0:26, :], state_d[b, 4:30, :]))

    def ada_inter(c):
        for cc in range(24 + 3 * c, min(24 + 3 * c + 3, 72)):
            ada_chunk(cc)

    if not ONLY_MD_G:
        ffn(0, w1[0], w3[0], w2[0], interleave=ada_inter)
        assert ada_stream.nxt == 72
    if DEBUG:
        dbg["x1"] = dout("dbg_x1", [NFT, 128, T])
        dbg["M"] = dout("dbg_M", [128, 720])
        kb.dma(sp, dbg["M"], Mt[:, :, :].rearrange("p a b -> p (a b)"), reads=[MB], writes=[outB], sem_buf=MB, partial=True)
    if not ONLY_MD_G:
        epilogue(dump=dbg.get("x1"))

    STOP = globals().get("STOP_AFTER", 99)
    mix_ok = True
    if STOP >= 2:
        if DEBUG:
            dbg["x2"] = dout("dbg_x2", [NFT, 128, T])
        mix_ok = mixer()
    if STOP >= 3 and mix_ok:
        ffn(2, w1[1], w3[1], w2[1])
        epilogue(final=True)
    while d2d_jobs:
        o_, i_ = d2d_jobs.pop(0)
        kb.dma(sp, o_, i_, reads=[], writes=[outB], sem_buf=d2dB, partial=True)
    kb.barrier()
    es.close()
    _LAST['n_ins'] = kb.n_ins
    _LAST['nsem'] = kb.nsem
    return nc


def _t5_buckets(dist):
    n = np.asarray(dist)
    max_exact = 16
    large = max_exact + (np.log(np.maximum(n, 1) / max_exact) / np.log(2048 / max_exact) * (32 - max_exact)).astype(np.int64)
    large = np.minimum(large, 31)
    return np.where(n < max_exact, n, large).astype(np.int32)


def _fm(v, nt):
    return np.ascontiguousarray(np.asarray(v, np.float32).reshape(nt, 128).T)


_LAST = {}


def kernel(x_prompt, x_sample, cache_k_w128, cache_v_w128, cache_k_w512, cache_v_w512,
           cache_k_w2048, cache_v_w2048, state_conv, c_prompt, c_sample,
           w_ada, b_ada, ffn1_norm_pre, ffn1_norm_post, ffn1_w1, ffn1_w3, ffn1_w2,
           mix_norm_pre, mix_norm_post, w_in, dw_kernel, dw_bias, conv_ln_g, conv_ln_b,
           w_conv_out, w_att_out, w_out, rel_bias,
           ffn2_norm_pre, ffn2_norm_post, ffn2_w1, ffn2_w3, ffn2_w2):
    f32 = np.float32
    A = lambda a: np.ascontiguousarray(np.asarray(a, f32))
    nc = build_program()

    ident = np.eye(128, dtype=f32)
    sel = np.zeros((32, 3 * 129), f32)
    for g, (w, d) in enumerate(GROUPS):
        bk = _t5_buckets(d * np.arange(129))
        sel[bk, g * 129 + np.arange(129)] = 1.0
    rowm = np.zeros((12, 3), f32)
    for r in range(12):
        rowm[r, r // 4] = 1.0
    masks = np.zeros((16, 32), f32)
    for p in range(16):
        for q in range(16):
            if p // 4 == q // 4:
                masks[p, q] = 1.0
        masks[p, 16 + p] = 1.0

    vecs = np.zeros((128, NVEC), f32)
    vecs[:, V_BADA:V_BADA + 144] = _fm(b_ada[0], 144)
    for i, nv in enumerate((ffn1_norm_pre, ffn1_norm_post, mix_norm_pre, mix_norm_post, ffn2_norm_pre, ffn2_norm_post)):
        vecs[:, V_NORM + 16 * i:V_NORM + 16 * i + 16] = _fm(nv[0], 16)
    vecs[:, V_DWK:V_DWK + 248] = np.asarray(dw_kernel[0], f32).reshape(31, 8, 128).transpose(2, 1, 0).reshape(128, 248)
    vecs[:, V_DWB:V_DWB + 8] = _fm(dw_bias[0], 8)
    vecs[:, V_LNG:V_LNG + 8] = _fm(conv_ln_g[0], 8)
    vecs[:, V_LNB:V_LNB + 8] = _fm(conv_ln_b[0], 8)

    xp = np.asarray(x_prompt, f32)[0]
    xs = np.asarray(x_sample, f32)
    cks = [np.asarray(c, f32)[0] for c in (cache_k_w128, cache_k_w512, cache_k_w2048)]
    cvs = [np.asarray(c, f32)[0] for c in (cache_v_w128, cache_v_w512, cache_v_w2048)]
    st = np.asarray(state_conv, f32)[0]
    shared = dict(
        vecs=vecs, ident=ident, sel=sel, relb=A(rel_bias), masks=masks, rowm=rowm,
        w_ada=A(w_ada[0]), w1a=A(ffn1_w1[0]), w3a=A(ffn1_w3[0]), w2a=A(ffn1_w2[0]),
        w_in=A(w_in[0]), w_co=A(w_conv_out[0]), w_ao=A(w_att_out[0]), w_out=A(w_out[0]),
        w1b=A(ffn2_w1[0]), w3b=A(ffn2_w3[0]), w2b=A(ffn2_w2[0]),
    )
    in_maps = []
    for c in range(NCORES):
        xin = np.zeros((T, D), f32)
        xin[0:TP] = xp[TP * c:TP * (c + 1)]
        if c > 0:
            xin[TP:TPH] = xp[TP * c - NHALO:TP * c]
        xin[TPH:T] = xs[4 * c:4 * c + 4].reshape(NS, D)
        c_all = np.concatenate([np.asarray(c_prompt, f32)[0:1], np.asarray(c_sample, f32)[4 * c:4 * c + 4]], 0)
        cT = np.ascontiguousarray(c_all.T.reshape(16, 128, 5).transpose(1, 0, 2).reshape(128, 80))
        flags = np.zeros((128, 4), f32)
        flags[:, 0] = 1.0 if c >= 1 else 0.0
        flags[:, 1] = 1.0 if c >= 2 else 0.0
        flags[:, 2] = 1.0 if c >= 1 else 0.0
        m = dict(shared)
        m.update(xin=xin, cT=cT, flags=flags, state=A(st[4 * c:4 * c + 4]))
        for g in range(3):
            L = GROUPS[g][0]
            m[f"ck{g}"] = A(cks[g][4 * c:4 * c + 4].reshape(4, L, 512))
            m[f"cv{g}"] = A(cvs[g][4 * c:4 * c + 4].reshape(4, L, 512))
        in_maps.append(m)

    res = run_bass_kernel_spmd(nc, in_maps, core_ids=list(range(NCORES)))
    R = res.results
    _LAST["res"] = R
    y_prompt = np.concatenate([R[c]["y"][0:TP] for c in range(NCORES)], 0)[None]
    y_sample = np.concatenate([R[c]["y"][TP:TP + NS].reshape(4, 4, D) for c in range(NCORES)], 0)
    outs = [y_prompt, y_sample]
    for g, (w, d) in enumerate(GROUPS):
        for nm in ("kout", "vout"):
            if w <= TP:
                rows = R[7][nm][g][TP - w:TP]
            else:
                rows = np.concatenate([R[6][nm][g], R[7][nm][g]], 0)
            outs.append(np.ascontiguousarray(rows.reshape(1, 1, w, 4, 128)))
    outs.append(np.ascontiguousarray(R[7]["pconv"].reshape(1, 1, NHALO, CONV_CH)))
    for g, (w, d) in enumerate(GROUPS):
        for nm in ("sk", "sv"):
            outs.append(np.concatenate([R[c][f"{nm}{g}"] for c in range(NCORES)], 0).reshape(1, 32, w, 4, 128))
    outs.append(np.concatenate([R[c]["sconv"] for c in range(NCORES)], 0).reshape(1, 32, 30, CONV_CH))
    return tuple(np.ascontiguousarray(o, dtype=f32) for o in outs)
```
